# Optimizing a Trainium2 kernel written in Bass

```python
import jax, jax.numpy as jnp
from jax import lax
import numpy as np


D_MODEL = 4096
BATCH = 32
SEQ = 256
DEPTH = 4
DEC_BATCH = 8
DEC_SEQ = 1024
PAST_LEN = 512

GRID_W = 64
N_MIXERS = 2
N_HEADS = 32
HEAD_DIM = D_MODEL // N_HEADS
WIN_H = 8
WIN_W = 16
POOL_WINDOWS = (2, 4, 8, 16)
N_POOL_GROUPS = 4
POOL_GROUP_DIM = D_MODEL // N_POOL_GROUPS
N_ATTN_LAYERS = (DEPTH + 1) // 2
N_POOL_LAYERS = DEPTH // 2
RMS_EPS = 1e-6
NEG_INF = -1e30

kernel_name = 'na_pool_prefix_dit_step'


def rmsnorm(x, g):
    xf = x.astype(jnp.float32)
    y = xf * lax.rsqrt(jnp.mean(xf * xf, axis=-1, keepdims=True) + RMS_EPS)
    return (y * g.astype(jnp.float32)).astype(x.dtype)


def modulation(cond, w_ada, b_ada):
    m = jax.nn.silu(cond) @ w_ada + b_ada
    return jnp.split(m, 3, axis=-1)


def modulate(x, g, shift, scale):
    return rmsnorm(x, g) * (1 + scale) + shift


def attn_context(h, w_in):
    B, T, _ = h.shape
    q, k, v, z = jnp.split(h @ w_in, 4, axis=-1)
    q = q.reshape(B, T, N_HEADS, HEAD_DIM)
    k = k.reshape(B, T, N_HEADS, HEAD_DIM)
    v = v.reshape(B, T, N_HEADS, HEAD_DIM)
    s = jnp.einsum('bqhd,bkhd->bhqk', q, k).astype(jnp.float32) * (HEAD_DIM ** -0.5)
    p = jax.nn.softmax(s, axis=-1).astype(v.dtype)
    o = jnp.einsum('bhqk,bkhd->bqhd', p, v).reshape(B, T, D_MODEL)
    return o * jax.nn.silu(z), k, v


def na_geometry(rows):
    kh = min(WIN_H, rows)
    r = np.arange(rows)
    rs = np.clip(r - kh // 2, 0, rows - kh)
    row_idx = rs[:, None] + np.arange(kh)[None, :]
    dr = row_idx - r[:, None]
    col = np.arange(GRID_W)
    cs = np.clip(col - WIN_W // 2, 0, GRID_W - WIN_W)
    col_in = (col[None, :] >= cs[:, None]) & (col[None, :] < cs[:, None] + WIN_W)
    dc = np.clip(col[None, :] - col[:, None], -(WIN_W - 1), WIN_W - 1)
    return kh, row_idx, dr, col_in, dc


def attn_latent(h, k_ctx, v_ctx, w_in, rpb):
    B, T, _ = h.shape
    rows = T // GRID_W
    kh, row_idx, dr, col_in, dc = na_geometry(rows)
    q, k, v, z = jnp.split(h @ w_in, 4, axis=-1)
    grid = (B, rows, GRID_W, N_HEADS, HEAD_DIM)
    q = q.reshape(grid)
    k = k.reshape(grid)
    v = v.reshape(grid)
    k_rows = k[:, row_idx]
    v_rows = v[:, row_idx]
    scale = HEAD_DIM ** -0.5
    bias = rpb[:, (dr + WIN_H - 1)[:, None, :, None], (dc + WIN_W - 1)[None, :, None, :]]
    s_win = jnp.einsum('brqhd,brkwhd->bhrqkw', q, k_rows).astype(jnp.float32) * scale + bias.astype(jnp.float32)[None]
    s_win = jnp.where(col_in[None, None, None, :, None, :], s_win, NEG_INF)
    s_ctx = jnp.einsum('brqhd,bchd->bhrqc', q, k_ctx).astype(jnp.float32) * scale
    n_win = kh * GRID_W
    s = jnp.concatenate([s_win.reshape(B, N_HEADS, rows, GRID_W, n_win), s_ctx], axis=-1)
    p = jax.nn.softmax(s, axis=-1).astype(v.dtype)
    p_win = p[..., :n_win].reshape(B, N_HEADS, rows, GRID_W, kh, GRID_W)
    p_ctx = p[..., n_win:]
    o = (jnp.einsum('bhrqkw,brkwhd->brqhd', p_win, v_rows)
         + jnp.einsum('bhrqc,bchd->brqhd', p_ctx, v_ctx.astype(v.dtype)))
    o = o.reshape(B, T, D_MODEL)
    return o * jax.nn.silu(z)


def multiscale_pool(u):
    B, T, _ = u.shape
    uf = u.astype(jnp.float32)
    csum = jnp.concatenate([jnp.zeros((B, 1, D_MODEL), jnp.float32), jnp.cumsum(uf, axis=1)], axis=1)
    t = np.arange(T)
    outs = []
    for g, w in enumerate(POOL_WINDOWS):
        lo = np.clip(t - w // 2, 0, T)
        hi = np.clip(t + w // 2, 0, T)
        sl = slice(g * POOL_GROUP_DIM, (g + 1) * POOL_GROUP_DIM)
        cg = csum[..., sl]
        mean = (cg[:, hi] - cg[:, lo]) / (hi - lo).astype(np.float32)[None, :, None]
        outs.append(mean - uf[..., sl])
    return jnp.stack(outs, axis=2)


def pool_mixer(h, w_in, w_grp, pool_scale):
    B, T, _ = h.shape
    u, z = jnp.split(h @ w_in, 2, axis=-1)
    d = multiscale_pool(u).astype(u.dtype)
    y = jnp.einsum('btgi,gio->btgo', d, w_grp).reshape(B, T, D_MODEL)
    return y * pool_scale * jax.nn.silu(z)


def setup_inputs(seed: int = 0) -> dict:
    key = jax.random.key(seed)
    ks = jax.random.split(key, 18)
    D = D_MODEL
    sd = D ** -0.5
    return {
        'x_prompt': jax.random.normal(ks[0], (BATCH, SEQ, D), jnp.float32),
        'x_sample': jax.random.normal(ks[1], (DEC_BATCH, DEC_SEQ, D), jnp.float32),
        'c': jax.random.normal(ks[2], (DEC_BATCH, D), jnp.float32),
        'cache_k': jax.random.normal(ks[3], (DEC_BATCH, N_ATTN_LAYERS, PAST_LEN, N_HEADS, HEAD_DIM), jnp.float32),
        'cache_v': jax.random.normal(ks[4], (DEC_BATCH, N_ATTN_LAYERS, PAST_LEN, N_HEADS, HEAD_DIM), jnp.float32),
        'c_ctx': jax.random.normal(ks[5], (D,), jnp.float32),
        'norm_g': 1.0 + 0.1 * jax.random.normal(ks[6], (DEPTH, D), jnp.float32),
        'w_ada': 0.5 * sd * jax.random.normal(ks[7], (DEPTH, D, 3 * D), jnp.float32),
        'b_ada': 0.02 * jax.random.normal(ks[8], (DEPTH, 3 * D), jnp.float32),
        'w_in_attn': sd * jax.random.normal(ks[9], (N_ATTN_LAYERS, D, 4 * D), jnp.float32),
        'rpb': 0.5 * jax.random.normal(ks[10], (N_ATTN_LAYERS, N_HEADS, 2 * WIN_H - 1, 2 * WIN_W - 1), jnp.float32),
        'w_out_attn': sd * jax.random.normal(ks[11], (N_ATTN_LAYERS, D, D), jnp.float32),
        'w_in_pool': sd * jax.random.normal(ks[12], (N_POOL_LAYERS, D, 2 * D), jnp.float32),
        'w_grp_pool': (POOL_GROUP_DIM ** -0.5) * jax.random.normal(ks[13], (N_POOL_LAYERS, N_POOL_GROUPS, POOL_GROUP_DIM, POOL_GROUP_DIM), jnp.float32),
        'pool_scale': 1.0 + 0.1 * jax.random.normal(ks[14], (N_POOL_LAYERS, D), jnp.float32),
        'w_out_pool': sd * jax.random.normal(ks[15], (N_POOL_LAYERS, D, D), jnp.float32),
        'final_norm_g': 1.0 + 0.1 * jax.random.normal(ks[16], (D,), jnp.float32),
    }


def reference(x_prompt, x_sample, c, cache_k, cache_v, c_ctx, norm_g, w_ada, b_ada,
              w_in_attn, rpb, w_out_attn, w_in_pool, w_grp_pool, pool_scale, w_out_pool,
              final_norm_g):
    xc = x_prompt
    xl = x_sample
    cond_ctx = c_ctx[None, None, :]
    cond_lat = c[:, None, :]
    new_k = []
    new_v = []
    for i in range(DEPTH):
        j = i // N_MIXERS
        sh_c, sc_c, gt_c = modulation(cond_ctx, w_ada[i], b_ada[i])
        sh_l, sc_l, gt_l = modulation(cond_lat, w_ada[i], b_ada[i])
        hc = modulate(xc, norm_g[i], sh_c, sc_c)
        hl = modulate(xl, norm_g[i], sh_l, sc_l)
        if i % N_MIXERS == 0:
            oc, kc, vc = attn_context(hc, w_in_attn[j])
            new_k.append(kc)
            new_v.append(vc)
            ol = attn_latent(hl, cache_k[:, j], cache_v[:, j], w_in_attn[j], rpb[j])
            w_out = w_out_attn[j]
        else:
            oc = pool_mixer(hc, w_in_pool[j], w_grp_pool[j], pool_scale[j])
            ol = pool_mixer(hl, w_in_pool[j], w_grp_pool[j], pool_scale[j])
            w_out = w_out_pool[j]
        xc = xc + gt_c * (oc @ w_out)
        xl = xl + gt_l * (ol @ w_out)
    y_prompt = rmsnorm(xc, final_norm_g)
    y_sample = rmsnorm(xl, final_norm_g)
    new_cache_k = jnp.stack(new_k, axis=1)
    new_cache_v = jnp.stack(new_v, axis=1)
    return (y_prompt, y_sample, new_cache_k, new_cache_v)
```

```python
import numpy as np
import concourse.bass as bass
import concourse.mybir as mybir
from concourse.bass_utils import run_bass_kernel_spmd

F32, BF16 = mybir.dt.float32, mybir.dt.bfloat16
AF = mybir.ActivationFunctionType
ALU = mybir.AluOpType

T = 1024
NT = 8
GRID_W, ROWS, WIN_H, WIN_W = 64, 16, 8, 16
PAST = 512
DEPTH = 4
EPS = 1e-6
NEG = -1e30
POOL_WINDOWS = (2, 4, 8, 16)
NS = 4
NJ = 22
ENGS = ("sync", "scalar", "vector", "gpsimd", "tensor")
DBG_STOP = 0


def _win_tiles():
    return [[0, 1, 2, 3, 4, 5], [2, 3, 4, 5, 6, 7]]


def _static_consts():
    r = np.arange(ROWS)
    rs = np.clip(r - WIN_H // 2, 0, ROWS - WIN_H)
    col = np.arange(GRID_W)
    cs = np.clip(col - WIN_W // 2, 0, GRID_W - WIN_W)
    col_in = (col[None, :] >= cs[:, None]) & (col[None, :] < cs[:, None] + WIN_W)
    m01 = np.zeros((12, 128, 512), np.float32)
    for qc in range(2):
        for ti, kc in enumerate(_win_tiles()[qc]):
            for krl in range(2):
                kr = 2 * kc + krl
                for qi in range(8):
                    qr = qc * 8 + qi
                    if rs[qr] <= kr < rs[qr] + WIN_H:
                        m01[qc * 6 + ti, krl * 64:(krl + 1) * 64, qi * 64:(qi + 1) * 64] = col_in.T
    def pm(w, L):
        t = np.arange(L)
        lo = np.clip(t - w // 2, 0, L)
        hi = np.clip(t + w // 2, 0, L)
        tp = np.arange(L)[:, None]
        m = ((tp >= lo[None, :]) & (tp < hi[None, :])).astype(np.float64) / (hi - lo)[None, :]
        m -= np.eye(L)
        return m.astype(np.float32)
    pml = np.zeros((4, 10, 128, 512), np.float32)
    pmc = np.zeros((4, 2, 128, 256), np.float32)
    for g, w in enumerate(POOL_WINDOWS):
        ml = pm(w, 1024)
        mc = pm(w, 256)
        for tb in range(2):
            for i, kb in enumerate(_pml_kbs(tb)):
                pml[g, tb * 5 + i] = ml[kb * 128:(kb + 1) * 128, tb * 512:(tb + 1) * 512]
        for kb in range(2):
            pmc[g, kb] = mc[kb * 128:(kb + 1) * 128, :]
    return m01, pml, pmc


def _pml_kbs(tb):
    return [0, 1, 2, 3, 4] if tb == 0 else [3, 4, 5, 6, 7]


def _rpb_expand(rpb):
    NL, H = rpb.shape[:2]
    krl = np.arange(2)[:, None, None, None]
    kcol = np.arange(64)[None, :, None, None]
    jj = np.arange(NJ)[None, None, :, None]
    qcol = np.arange(64)[None, None, None, :]
    drt = 10 + krl - jj
    cs = np.clip(qcol - WIN_W // 2, 0, GRID_W - WIN_W)
    colok = (kcol >= cs) & (kcol < cs + WIN_W)
    dri = np.clip(drt + WIN_H - 1, 0, 2 * WIN_H - 2)
    dci = np.clip(kcol - qcol, -(WIN_W - 1), WIN_W - 1) + WIN_W - 1
    shape = (2, 64, NJ, 64)
    dri, dci = np.broadcast_to(dri, shape), np.broadcast_to(dci, shape)
    g = rpb[:, :, dri, dci]
    outs = []
    for lo, hi in ((-(WIN_H // 2), WIN_H // 2 - 1), (-(WIN_H - 1), WIN_H - 1)):
        valid = np.broadcast_to((drt >= lo) & (drt <= hi) & colok, shape)
        outs.append(np.where(valid[None, None], g, np.float32(NEG)).astype(np.float32))
    out = np.stack(outs, 2)
    return np.ascontiguousarray(out.reshape(NL, H, 2, 128, NJ * 64))


def _win_variant(qc, kc):
    return 1 if (qc, kc) in ((0, 2), (0, 3), (1, 4), (1, 5)) else 0


class Cx:
    def __init__(self, nc, sems, cur, handle):
        self.nc, self.sems, self.cur, self.h = nc, sems, cur, handle
        self.cnt = {}
        self.waited = {}
        self.tile_idx = 0
        self.bank_free = [[] for _ in range(8)]
        self.slot_free = {}
        self.tile_src = {}

    def _wait(self, tok):
        if tok is None:
            return
        s, v = tok
        if self.waited.get(s, 0) >= v:
            return
        self.waited[s] = v
        self.h.wait_ge(self.sems[s], v)

    def op(self, eng, fn, waits=(), sig=True):
        tok = None
        if sig:
            self.cnt[eng] = self.cnt.get(eng, 0) + 1
            tok = (eng, self.cnt[eng])
        if eng == self.cur:
            for w in waits:
                self._wait(w)
            inst = fn(self.h)
            if sig:
                inst.then_inc(self.sems[eng], 1)
        return tok

    STORE_SEMS = ("ost", "gst", "xst0", "xst1", "fst0", "fst1", "gss")

    def dma(self, eng, out, in_, sem, waits=()):
        if eng == "sync" and sem not in self.STORE_SEMS:
            waits = list(waits) + [self.total(x) for x in self.STORE_SEMS]
        self.cnt[sem] = self.cnt.get(sem, 0) + 16
        tok = (sem, self.cnt[sem])
        if eng == self.cur:
            for w in waits:
                self._wait(w)
            self.h.dma_start(out=out, in_=in_).then_inc(self.sems[sem], 16)
        return tok

    def total(self, sem):
        c = self.cnt.get(sem, 0)
        return (sem, c) if c else None


SEM_NAMES = (list(ENGS) + [f"ring{i}" for i in range(NS)] +
             ["xt0", "xt1", "xst0", "xst1", "ld0", "ld1", "ldc", "cst", "bh0", "bh1", "gst", "ost", "fst0", "fst1", "misc", "ldo", "ldm", "ldi", "bhs", "gss"])


def build(D):
    KC = D // 128
    H = KC
    NU = D // 512
    KT = min(8, KC)
    Dg = D // 4
    NBG = Dg // 128
    UW = min(512, Dg)
    NUG = Dg // UW
    NHG = H // 4
    SCALE = 128.0 ** -0.5

    nc = bass.Bass("TRN2", target_bir_lowering=False)
    dt = lambda n, s, k="ExternalInput", d=F32: nc.dram_tensor(n, list(s), d, kind=k)
    xl_in = dt("xl", [T, D]).ap()
    xc_in = dt("xc", [T, D]).ap()
    condT_d = dt("condT", [128, KC * 2]).ap()
    ck_d = dt("ck", [2 * PAST, D]).ap()
    cv_d = dt("cv", [2 * PAST, D]).ap()
    ngfm_d = dt("ngfm", [128, 4 * KC]).ap()
    fgbc_d = dt("fgbc", [128, D]).ap()
    badafm_d = dt("badafm", [128, 4 * 3 * KC]).ap()
    badag_d = dt("badag", [2, 4 * D]).ap()
    psfm_d = dt("psfm", [128, 2 * KC]).ap()
    rpbx_d = dt("rpbx", [2 * H * 2 * 128, NJ * 64]).ap()
    ident_d = dt("identc", [128, 128]).ap()
    ones_d = dt("onesc", [128, 128]).ap()
    pml_d = dt("pmlc", [40 * 128, 512]).ap()
    pmc_d = dt("pmcc", [8 * 128, 256]).ap()
    BR = KT * 128
    blocks = lambda n, rows, cols: [dt(f"{n}_{b}", [BR, cols]).ap() for b in range(rows // BR)]
    w_ada_d = blocks("w_ada", 4 * D, 3 * D)
    w_in_attn_d = blocks("w_in_attn", 2 * D, 4 * D)
    w_out_attn_d = blocks("w_out_attn", 2 * D, D)
    w_in_pool_d = blocks("w_in_pool", 2 * D, 2 * D)
    w_grp_d = dt("w_grp_pool", [2 * 4 * Dg, Dg]).ap()
    w_out_pool_d = blocks("w_out_pool", 2 * D, D)
    yl_d = dt("yl", [T, D], "ExternalOutput").ap()
    yc_d = dt("yc", [T, D], "ExternalOutput").ap()
    nk_d = dt("nk", [4 * 2 * 256, D], "ExternalOutput").ap()
    nv_d = dt("nv", [4 * 2 * 256, D], "ExternalOutput").ap()
    xs_d = [dt("xls", [T, D], "Internal").ap(), dt("xcs", [T, D], "Internal").ap()]
    gts_d = dt("gts", [128, KC * T], "Internal", BF16).ap()
    gate_s = dt("gates", [4 * 2, D], "Internal").ap()

    ARENA = 96 * 1024
    from contextlib import ExitStack
    es = ExitStack()
    sb = lambda n, s, d: es.enter_context(nc.sbuf_tensor(n, list(s), d))
    A = sb("A", [128, KC, T], BF16)
    ring = sb("ring", [128, NS, KT, 512], BF16)
    arena = sb("arena", [128, ARENA // 2], BF16)
    ident = sb("ident", [128, 128], F32)
    ones = sb("ones", [128, 128], BF16)
    identb = sb("identb", [128, 128], BF16)
    condT = sb("condTs", [128, KC * 2], F32)
    scT = sb("scT", [128, KC, 2], BF16)
    shT = sb("shT", [128, 4, 2, KC], F32)
    g1sT = sb("g1sT", [128, 4, 2, KC], F32)
    ngfm = sb("ngfms", [128, 4, KC], F32)
    badafm = sb("badafms", [128, 4, 3 * KC], F32)
    bp1 = sb("bp1", [128, 4, KC], F32)
    psfm = sb("psfms", [128, 2, KC], F32)
    gsm = sb("gsm", [2, 512], F32)
    ss = sb("ss", [128, 8], F32)
    sd = sb("sd", [128, 8], F32)
    rstd = sb("rstd", [128, 8], F32)
    psum = es.enter_context(nc.psum_tensor("psum", [128, 8, 512], F32))
    sems = {n: es.enter_context(nc.semaphore(n)) for n in SEM_NAMES}

    def carve(off, shape, dtype):
        n = int(np.prod(shape))
        if dtype == BF16:
            v = arena[:, off // 2: off // 2 + n]
        else:
            v = arena[:, off // 2: off // 2 + 2 * n].bitcast(F32)
        if len(shape) == 1:
            return v
        names = " ".join(f"d{i}" for i in range(len(shape)))
        return v.rearrange(f"p ({names}) -> p {names}", **{f"d{i}": shape[i] for i in range(1, len(shape))})

    K = 1024
    xt = [carve(0, [D], F32), carve(16 * K, [D], F32)]
    xeb = [carve(0, [8, 512], F32), carve(16 * K, [8, 512], F32)]
    junk = carve(32 * K, [D], BF16)
    fgbc = carve(40 * K, [D], F32)
    gatebc = carve(32 * K, [D], F32)
    tmpe = [carve(48 * K, [512], F32), carve(50 * K, [512], F32)]
    gbias = carve(52 * K, [D], F32)
    qT = carve(0, [4, T], BF16)
    kT = carve(8 * K, [4, T + PAST], BF16)
    szT = carve(20 * K, [4, T], BF16)
    vtm = carve(28 * K, [12, 512], BF16)
    gst = carve(40 * K, [4, T], BF16)
    stmp = [carve(48 * K, [512], F32), carve(50 * K, [512], F32)]
    ebuf = [carve(52 * K, [512], BF16), carve(53 * K, [512], BF16)]
    pbuf = [carve(54 * K, [512], BF16), carve(55 * K, [512], BF16), carve(56 * K, [512], BF16)]
    rden = carve(57 * K, [512], F32)
    t1 = carve(59 * K, [512], F32)
    cst = carve(61 * K, [4, 512], F32)
    bhs = carve(69 * K, [2, NJ, 64], F32)
    bhb = [carve(69 * K + 2 * NJ * 256, [2, NJ * 64], BF16), carve(69 * K + 3 * NJ * 256, [2, NJ * 64], BF16)]
    ost = carve(61 * K, [8, 512], F32)
    utm = carve(0, [8, UW], BF16)
    dT = carve(8 * K, [NBG, T], BF16)
    szg = carve(24 * K, [NBG, T], BF16)
    gstp = carve(40 * K, [4, T], BF16)
    pml = carve(48 * K, [10, 512], BF16)
    pmc = carve(58 * K, [2, 256], BF16)
    gsb = carve(0, [D], F32)
    bg2 = carve(16 * K, [D], F32)

    def wview(w2d):
        return w2d.rearrange("(kc p) n -> p kc n", p=128)

    def program(cx):
        E = cx
        bank = lambda b: psum[:, b, :]

        def issue_tile(k):
            if k not in cx.tile_src:
                return
            src, nk, ncols = cx.tile_src[k]
            slot = k % NS
            E.dma("gpsimd", ring[:, slot, 0:nk, 0:ncols], src, f"ring{slot}",
                  waits=[cx.slot_free.get(k - NS)])

        def tile_tok(k):
            return (f"ring{k % NS}", 16 * (k // NS + 1))

        def gemm(w2d, r0, c0, ncols, nkc, naccs, mm_fn, evac_fn):
            blocked = isinstance(w2d, list)
            wv = None if blocked else wview(w2d)
            kc0 = r0 // 128
            ntile = (nkc + KT - 1) // KT
            acc_tok = [None] * naccs
            for kt in range(ntile):
                nk = min(KT, nkc - kt * KT)
                n = cx.tile_idx
                cx.tile_idx += 1
                if cx.cur is None:
                    if blocked:
                        src = wview(w2d[r0 // BR + kt])[:, 0:nk, c0:c0 + ncols]
                    else:
                        src = wv[:, kc0 + kt * KT: kc0 + kt * KT + nk, c0:c0 + ncols]
                    TILE_LIST[n] = (src, nk, ncols)
                if n == 0:
                    for k in range(NS):
                        issue_tile(k)
                slot = n % NS
                ttok = None
                for k in range(nk):
                    kc = kt * KT + k
                    for bi in range(naccs):
                        waits = []
                        if k == 0 and bi == 0:
                            waits.append(tile_tok(n))
                        if kc == 0:
                            waits += cx.bank_free[bi]
                            cx.bank_free[bi] = []
                        last_tile = (k == nk - 1 and bi == naccs - 1)
                        last_kc = (kc == nkc - 1)
                        tok = E.op("tensor",
                                   lambda t, bi=bi, kc=kc, k=k, slot=slot, last_kc=last_kc:
                                   mm_fn(t, bi, ring[:, slot, k, :], kc, kc == 0, last_kc),
                                   waits, sig=(last_tile or last_kc))
                        if last_kc:
                            acc_tok[bi] = tok
                        if last_tile:
                            ttok = tok
                cx.slot_free[n] = ttok
                issue_tile(n + NS)
            for bi in range(naccs):
                cx.bank_free[bi] = evac_fn(bi, acc_tok[bi])
            ada_tick()

        def gemm_B(w2d, r0, c0, ncols, nkc, rhs_fn, evac_fn, ntb=2, nfree=512):
            nm = ncols // 128
            def mm(t, bi, wt, kc, st, sp):
                m, tb = bi // ntb, bi % ntb
                return t.matmul(psum[:, bi, 0:nfree], wt[:, m * 128:(m + 1) * 128], rhs_fn(kc, tb), start=st, stop=sp)
            gemm(w2d, r0, c0, ncols, nkc, nm * ntb, mm, lambda bi, tok: evac_fn(bi // ntb, bi % ntb, bi, tok))

        def gemm_A(w2d, r0, c0, ncols, nkc, lhs_fn, evac_fn):
            def mm(t, bi, wt, kc, st, sp):
                return t.matmul(psum[:, bi, 0:ncols], lhs_fn(kc, bi), wt[:, 0:ncols], start=st, stop=sp)
            gemm(w2d, r0, c0, ncols, nkc, NT, mm, evac_fn)

        _alt = [0]
        def copy_evac(out, in_, tok, scale=None, extra=()):
            _alt[0] ^= 1
            if _alt[0]:
                if scale is None:
                    return E.op("vector", lambda v: v.tensor_copy(out, in_), [tok, *extra])
                return E.op("vector", lambda v: v.tensor_scalar(out, in_, scale, None, ALU.mult), [tok, *extra])
            return E.op("scalar", lambda s: s.activation(out=out, in_=in_, func=AF.Copy,
                                                          scale=(1.0 if scale is None else scale)), [tok, *extra])

        c_tok = []
        for o, i_ in ((ident[:], ident_d), (condT[:], condT_d), (ngfm[:].rearrange("p a b -> p (a b)"), ngfm_d),
                      (badafm[:].rearrange("p a b -> p (a b)"), badafm_d), (psfm[:].rearrange("p a b -> p (a b)"), psfm_d)):
            c_tok.append(E.dma("sync", o, i_, "misc"))
        c_all = E.total("misc")
        ones_tok = E.dma("gpsimd", ones[:], ones_d, "ldo")
        identb_tok = E.dma("gpsimd", identb[:], ident_d, "ldi")
        t_sc = E.op("scalar", lambda s: s.activation(out=scT[:].rearrange("p a b -> p (a b)"), in_=condT[:], func=AF.Silu), [c_all])
        t_bp1 = E.op("vector", lambda v: v.tensor_scalar(bp1[:], badafm[:, :, KC:2 * KC], 1.0, None, ALU.add), [c_all])

        def finish():
            if cx.cur == "sync":
                for s_ in Cx.STORE_SEMS:
                    cx._wait(E.total(s_))

        if DBG_STOP == 1:
            return finish()
        gsm_free = [None]
        cx.bank_free[0] = [t_sc]

        def ada_layer(i):
            wa = w_ada_d
            for u in range(2 * NU):
                def rhs_fn(kc, tb):
                    return scT[:, kc, :]
                def evac(m, tb, bi, tok, u=u, i=i):
                    blk = u * 4 + m
                    if blk < KC:
                        return [E.op("vector", lambda v: v.tensor_scalar(
                            shT[:, i, :, blk], psum[:, bi, 0:2], badafm[:, i, blk:blk + 1], None, ALU.add), [tok, c_all])]
                    c = blk - KC
                    return [E.op("vector", lambda v: v.tensor_scalar(
                        g1sT[:, i, :, c], psum[:, bi, 0:2], bp1[:, i, c:c + 1], ngfm[:, i, c:c + 1], ALU.add, ALU.mult),
                        [tok, t_bp1, c_all])]
                gemm_B(wa, i * D, u * 512, 512, KC, rhs_fn, evac, ntb=1, nfree=2)
                yield
            for u in range(NU):
                def mm(t, bi, wt, kc, st, sp):
                    return t.matmul(psum[0:2, bi, :], scT[:, kc, :], wt[:, 0:512], start=st, stop=sp)
                def evac(bi, tok, u=u):
                    tk = E.op("vector", lambda v: v.tensor_copy(gsm[0:2, :], psum[0:2, bi, :]), [tok, gsm_free[0]])
                    gsm_free[0] = E.dma("sync", gate_s[2 * i:2 * i + 2, u * 512:(u + 1) * 512], gsm[0:2, :], "gss", waits=[tk])
                    return [tk]
                gemm(wa, i * D, 2 * D + u * 512, 512, KC, 1, mm, evac)
                yield

        ada_state = {"gen": None, "n": 0, "busy": False}

        def ada_tick():
            if ada_state["busy"] or ada_state["gen"] is None:
                return
            ada_state["n"] += 1
            if ada_state["n"] % 2:
                return
            ada_state["busy"] = True
            if next(ada_state["gen"], "end") == "end":
                ada_state["gen"] = None
            ada_state["busy"] = False

        def ada_flush():
            ada_state["busy"] = True
            if ada_state["gen"] is not None:
                for _ in ada_state["gen"]:
                    pass
            ada_state["gen"] = None
            ada_state["busy"] = False

        ada_state["gen"] = ada_layer(0)
        ada_flush()
        mod_ready = [E.total("vector"), E.total("gst")]
        if DBG_STOP == 2:
            return finish()


        def norm_phase(i, r, xsrc):
            xfree = [None, None]
            ht_toks = []
            for t in range(NT):
                b = t % 2
                lt = E.dma("sync", xt[b], xsrc[t * 128:(t + 1) * 128, :], f"xt{b}",
                           waits=[xfree[b]] + (([E.total("vector"), E.total("gst"), E.total("xst0"), E.total("xst1"), E.total("gpsimd")]) if t < 2 else []))
                a1 = E.op("scalar", lambda s: s.activation(out=junk, in_=xt[b], func=AF.Square,
                                                            accum_out=ss[:, t:t + 1]), [lt])
                a2 = E.op("scalar", lambda s: s.activation(out=sd[:, t:t + 1], in_=ss[:, t:t + 1], func=AF.Sqrt,
                                                            bias=EPS_AP[0], scale=1.0 / D), [a1, eps_tok])
                if DBG_STOP == 31:
                    continue
                v1 = E.op("vector", lambda v: v.reciprocal(rstd[:, t:t + 1], sd[:, t:t + 1]), [a2])
                v2 = E.op("vector", lambda v: v.tensor_scalar(xt[b], xt[b], rstd[:, t:t + 1], None, ALU.mult), [v1])
                if DBG_STOP == 32:
                    xfree[b] = v2
                    continue
                last = []
                for c4 in range(KC // 4):
                    bk = (t * (KC // 4) + c4) % 8
                    waits = [v2, c_all] + cx.bank_free[bk]
                    cx.bank_free[bk] = []
                    for q in range(4):
                        c = c4 * 4 + q
                        pt = E.op("tensor", lambda tt, c=c, q=q, bk=bk: tt.transpose(
                            psum[:, bk, q * 128:(q + 1) * 128], xt[b][:, c * 128:(c + 1) * 128], ident[:]),
                            waits if q == 0 else [], sig=(q == 3))
                    if DBG_STOP == 33:
                        continue
                    fr = []
                    for q in range(4):
                        c = c4 * 4 + q
                        o = A[:, c, t * 128:(t + 1) * 128]
                        i_ = psum[:, bk, q * 128:(q + 1) * 128]
                        if bk % 2 == 0:
                            fr.append(E.op("scalar", lambda s, o=o, i_=i_, c=c: s.activation(
                                out=o, in_=i_, func=AF.Identity, scale=g1sT[:, i, r, c:c + 1], bias=shT[:, i, r, c:c + 1]), [pt]))
                        else:
                            fr.append(E.op("vector", lambda v, o=o, i_=i_, c=c: v.tensor_scalar(
                                o, i_, g1sT[:, i, r, c:c + 1], shT[:, i, r, c:c + 1], ALU.mult, ALU.add), [pt]))
                    cx.bank_free[bk] = fr
                    last = [pt]
                    ht_toks = fr
                xfree[b] = pt
            if DBG_STOP in (31, 32, 33):
                return []
            return [E.total("scalar"), E.total("vector")]

        def outproj_phase(i, r, w2d, r0, xsrc, gts_ready):
            xdst = xs_d[r]
            lt = E.dma("sync", A[:].rearrange("p a b -> p (a b)"), gts_d, "ld0", waits=gts_ready)
            gl0 = E.dma("sync", gatebc, bass.AP(gate_s.tensor, (2 * i + r) * D, [[0, 128], [1, D]]), "ld1",
                        waits=[E.total("vector"), E.total("gpsimd")])
            gb = E.dma("sync", gbias, bass.AP(badag_d.tensor, i * D, [[0, 128], [1, D]]), "ldm")
            gl = E.op("gpsimd", lambda g: g.tensor_tensor(gatebc, gatebc, gbias, ALU.add), [gl0, gb])
            cx.bank_free[0] = cx.bank_free[0] + [lt]
            xe_free = [None, None]
            xl_tok = {}
            def load_xe(cu):
                b = cu % 2
                xl_tok[cu] = E.dma("sync", xeb[b],
                                   xsrc.rearrange("(t p) n -> p t n", p=128)[:, :, cu * 512:(cu + 1) * 512],
                                   f"xt{b}", waits=[xe_free[b]])
            load_xe(0)
            st_toks = [None, None]
            fin = []
            for cu in range(NU):
                b = cu % 2
                if cu + 1 < NU:
                    load_xe(cu + 1)
                xe = xeb[b]
                def lhs_fn(kc, t):
                    return A[:, kc, t * 128:(t + 1) * 128]
                def evac(bi, tok, cu=cu, xe=xe):
                    tb_ = bi % 2
                    d1 = E.op("vector", lambda v: v.tensor_tensor(tmpe[tb_], psum[:, bi, :], gatebc[:, cu * 512:(cu + 1) * 512], ALU.mult),
                              [tok, gl, lt] + fin[-2:-1])
                    p1 = E.op("gpsimd", lambda g: g.tensor_tensor(xe[:, bi, :], tmpe[tb_], xe[:, bi, :], ALU.add), [d1, xl_tok[cu]])
                    fin.append(p1)
                    return [d1]
                gemm_A(w2d, r0, cu * 512, 512, KC, lhs_fn, evac)
                st = E.dma("sync", xdst.rearrange("(t p) n -> p t n", p=128)[:, :, cu * 512:(cu + 1) * 512], xe,
                           f"xst{b}", waits=[fin[-1]])
                xe_free[b] = st
            return [E.total("xst0"), E.total("xst1")]

        gst_free = [None]
        ost_free = [None]
        bh_free = [None, None]
        bhs_free = [None]

        def attn_phase(i, r, hready):
            j = i // 2
            win = w_in_attn_d
            for hg in range(NHG):
                def rhsA(kc, tb):
                    return A[:, kc, tb * 512:(tb + 1) * 512]
                def ev_q(m, tb, bi, tok):
                    return [copy_evac(qT[:, m, tb * 512:(tb + 1) * 512], psum[:, bi, :], tok, scale=SCALE)]
                def ev_k(m, tb, bi, tok):
                    return [copy_evac(kT[:, m, tb * 512:(tb + 1) * 512], psum[:, bi, :], tok)]
                def ev_z(m, tb, bi, tok):
                    return [E.op("scalar", lambda s: s.activation(out=szT[:, m, tb * 512:(tb + 1) * 512], in_=psum[:, bi, :], func=AF.Silu), [tok])]
                def lhsA(kc, t):
                    return A[:, kc, t * 128:(t + 1) * 128]
                def ev_v(bi, tok):
                    if r == 1:
                        return [E.op("vector", lambda v: v.tensor_copy(vtm[:, bi, :], psum[:, bi, :]), [tok]),
                                E.op("vector", lambda v: v.tensor_copy(ost[:, bi, :], psum[:, bi, :]), [tok, ost_free[0]])]
                    return [copy_evac(vtm[:, bi, :], psum[:, bi, :], tok)]
                def store_ost(dst):
                    toks = []
                    for s in range(4):
                        toks.append(E.dma("sync", dst[(s * 2 + j) * 256:(s * 2 + j + 1) * 256, hg * 512:(hg + 1) * 512]
                                          .rearrange("(t p) n -> p t n", p=128), ost[:, 2 * s:2 * s + 2, :], "ost",
                                          waits=[E.total("vector"), E.total("scalar")]))
                    ost_free[0] = toks[-1]
                    return toks
                gemm_B(win, j * D, hg * 512, 512, KC, rhsA, ev_q)
                gemm_B(win, j * D, D + hg * 512, 512, KC, rhsA, ev_k)
                gemm_B(win, j * D, 3 * D + hg * 512, 512, KC, rhsA, ev_z)
                gemm_A(win, j * D, 2 * D + hg * 512, 512, KC, lhsA, ev_v)
                if r == 1:
                    store_ost(nv_d)
                    def ev_kt(bi, tok):
                        return [E.op("vector", lambda v: v.tensor_copy(ost[:, bi, :], psum[:, bi, :]), [tok, ost_free[0]])]
                    gemm_A(win, j * D, D + hg * 512, 512, KC, lhsA, ev_kt)
                    store_ost(nk_d)
                proj_done = [E.total("scalar"), E.total("vector")]
                jobs = []
                if r == 0:
                    lk = E.dma("sync", cst, ck_d[j * PAST:(j + 1) * PAST, hg * 512:(hg + 1) * 512].rearrange("(c p) n -> p c n", p=128),
                               "cst", waits=proj_done)
                    for m in range(4):
                        bk = 7
                        w_ = [lk] + cx.bank_free[bk]
                        cx.bank_free[bk] = []
                        for c in range(4):
                            pt = E.op("tensor", lambda tt, c=c, m=m: tt.transpose(psum[:, 7, c * 128:(c + 1) * 128],
                                                                                  cst[:, c, m * 128:(m + 1) * 128], ident[:]),
                                      w_ if c == 0 else [], sig=(c == 3))
                        cx.bank_free[bk] = [copy_evac(kT[:, m, T:T + PAST], psum[:, 7, :], pt)]
                    kc_done = [E.total("scalar"), E.total("vector")]
                    lv = E.dma("sync", cst, cv_d[j * PAST:(j + 1) * PAST, hg * 512:(hg + 1) * 512].rearrange("(c p) n -> p c n", p=128),
                               "cst", waits=kc_done + [E.total("tensor")])
                    cvt = E.op("vector", lambda v: v.tensor_copy(vtm[:, 8:12, :], cst), [lv])
                    for m in range(4):
                        for qc in range(2):
                            for ti, kc in enumerate(_win_tiles()[qc]):
                                jobs.append(dict(m=m, g=(m, qc), kc=kc, q0=qc * 512, nq=512, win=qc * 6 + ti,
                                                 jj0=10 - 2 * kc + 8 * qc, ko=kc * 128, var=_win_variant(qc, kc)))
                            for c in range(4):
                                jobs.append(dict(m=m, g=(m, qc), kc=8 + c, q0=qc * 512, nq=512, win=None, ko=T + c * 128))
                else:
                    cvt = None
                    for m in range(4):
                        for s in range(4):
                            for c in range(2):
                                jobs.append(dict(m=m, g=(m, s), kc=2 * s + c, q0=s * 256, nq=256, win=None, ko=(2 * s + c) * 128))
                PV_tok, last_bh = {}, {}
                fin_last = [None]
                bh_tok = {}
                def load_bh(m):
                    hh = hg * 4 + m
                    r0_ = (j * H + hh) * 2 * 128
                    ld = E.dma("sync", bhs, rpbx_d[r0_:r0_ + 256, :].rearrange("(v p) (a b) -> p v a b", p=128, a=NJ), "bhs",
                               waits=[bhs_free[0]])
                    bh_tok[m] = E.op("gpsimd", lambda g: g.tensor_copy(bhb[m % 2], bhs.rearrange("p v a b -> p v (a b)")),
                                     [ld, bh_free[m % 2]])
                    bhs_free[0] = bh_tok[m]
                if r == 0:
                    load_bh(0)
                nj = len(jobs)
                S_tok = [None] * nj
                P_tok = [None] * nj
                grp_idx = {}
                for jb in jobs:
                    grp_idx.setdefault(jb["g"], len(grp_idx))
                def emit_S(x):
                    jb = jobs[x]
                    bk = x % 3
                    w_ = proj_done + cx.bank_free[bk] + ([cvt] if cvt else []) + ([E.total("scalar"), E.total("vector")] if x == 0 else [])
                    cx.bank_free[bk] = []
                    iswin = jb["win"] is not None
                    S_tok[x] = E.op("tensor", lambda tt: tt.matmul(psum[:, bk, 0:jb["nq"]], kT[:, jb["m"], jb["ko"]:jb["ko"] + 128],
                                                                   qT[:, jb["m"], jb["q0"]:jb["q0"] + jb["nq"]], start=True, stop=not iswin),
                                    w_, sig=not iswin)
                    if iswin:
                        m = jb["m"]
                        if m not in bh_tok:
                            load_bh(m)
                        c0_ = jb["jj0"] * 64
                        S_tok[x] = E.op("tensor", lambda tt: tt.matmul(psum[:, bk, :], identb[:], bhb[m % 2][:, jb["var"], c0_:c0_ + 512],
                                                                       start=False, stop=True), [bh_tok[m], identb_tok])
                        last_bh[m] = S_tok[x]
                def emit_elem(x):
                    jb = jobs[x]
                    bk = x % 3
                    nq = jb["nq"]
                    pb = pbuf[x % 3]
                    pfree = PV_tok.get(x - 3)
                    a1 = E.op("scalar", lambda s: s.activation(out=pb[:, 0:nq], in_=psum[:, bk, 0:nq], func=AF.Exp), [S_tok[x], pfree])
                    P_tok[x] = a1
                    cx.bank_free[bk] = [a1]
                def emit_PV(x):
                    jb = jobs[x]
                    gi = grp_idx[jb["g"]]
                    nq = jb["nq"]
                    ob, db = 3 + gi % 2, 5 + gi % 2
                    first = (x == 0 or jobs[x - 1]["g"] != jb["g"])
                    lastj = (x == nj - 1 or jobs[x + 1]["g"] != jb["g"])
                    w_ = [P_tok[x]]
                    if first:
                        w_ += cx.bank_free[ob] + cx.bank_free[db]
                        cx.bank_free[ob] = []
                        cx.bank_free[db] = []
                    pb = pbuf[x % 3]
                    E.op("tensor", lambda tt: tt.matmul(psum[:, ob, 0:nq], vtm[:, jb["kc"], jb["m"] * 128:(jb["m"] + 1) * 128],
                                                        pb[:, 0:nq], start=first, stop=lastj), w_, sig=False)
                    PV_tok[x] = E.op("tensor", lambda tt: tt.matmul(psum[:, db, 0:nq], ones[:], pb[:, 0:nq], start=first, stop=lastj),
                                     [ones_tok])
                    if lastj:
                        m, q0 = jb["m"], jb["q0"]
                        f1 = E.op("vector", lambda v: v.reciprocal(rden[:, 0:nq], psum[:, db, 0:nq]), [PV_tok[x], fin_last[0]])
                        f2 = E.op("vector", lambda v: v.tensor_tensor(t1[:, 0:nq], psum[:, ob, 0:nq], rden[:, 0:nq], ALU.mult), [f1, fin_last[0]])
                        f3 = E.op("gpsimd", lambda g: g.tensor_tensor(gst[:, m, q0:q0 + nq], t1[:, 0:nq], szT[:, m, q0:q0 + nq], ALU.mult),
                                  [f2, gst_free[0], proj_done[0]])
                        fin_last[0] = f3
                        cx.bank_free[ob] = [f2]
                        cx.bank_free[db] = [f1]
                        if r == 0 and (m, 1) == jb["g"]:
                            bh_free[m % 2] = last_bh[m]
                emit_S(0)
                if nj > 1:
                    emit_S(1)
                for x in range(nj):
                    emit_elem(x)
                    emit_PV(x)
                    if x + 2 < nj:
                        emit_S(x + 2)
                gst_free[0] = E.dma("sync", gts_d.rearrange("p (a b) -> p a b", a=KC)[:, hg * 4:(hg + 1) * 4, :], gst, "gst",
                                    waits=[fin_last[0]])
            return [E.total("gst")]

        def pool_phase(i, r, hready):
            j = i // 2
            win = w_in_pool_d
            pm_free = [None]
            for g in range(4):
                pmw = [E.total("tensor"), pm_free[0]]
                if r == 0:
                    pm_tok = E.dma("gpsimd", pml, pml_d[g * 1280:(g + 1) * 1280, :].rearrange("(a p) n -> p a n", p=128), "ldc", waits=pmw)
                else:
                    pm_tok = E.dma("gpsimd", pmc, pmc_d[g * 256:(g + 1) * 256, :].rearrange("(a p) n -> p a n", p=128), "ldc", waits=pmw)
                for uu in range(NUG):
                    f0 = g * Dg + uu * UW
                    nfb = UW // 128
                    def lhsA(kc, t):
                        return A[:, kc, t * 128:(t + 1) * 128]
                    def ev_u(bi, tok):
                        return [copy_evac(utm[:, bi, :], psum[:, bi, 0:UW], tok)]
                    gemm_A(win, j * D, f0, UW, KC, lhsA, ev_u)
                    u_done = [E.total("scalar"), E.total("vector")]
                    for fb in range(nfb):
                        for tb in range(2):
                            bk = fb * 2 + tb
                            w_ = u_done + [pm_tok] + cx.bank_free[bk]
                            cx.bank_free[bk] = []
                            if r == 0:
                                kbs = _pml_kbs(tb)
                                for x, kb in enumerate(kbs):
                                    pt = E.op("tensor", lambda tt, x=x, kb=kb: tt.matmul(
                                        psum[:, bk, :], utm[:, kb, fb * 128:(fb + 1) * 128], pml[:, tb * 5 + x, :],
                                        start=(x == 0), stop=(x == len(kbs) - 1)), w_ if x == 0 else [], sig=(x == len(kbs) - 1))
                            else:
                                for sh in range(2):
                                    s = tb * 2 + sh
                                    for kb in range(2):
                                        pt = E.op("tensor", lambda tt, s=s, kb=kb, sh=sh: tt.matmul(
                                            psum[:, bk, sh * 256:(sh + 1) * 256], utm[:, 2 * s + kb, fb * 128:(fb + 1) * 128], pmc[:, kb, :],
                                            start=(kb == 0), stop=(kb == 1)), w_ if (sh == 0 and kb == 0) else [], sig=(sh == 1 and kb == 1))
                            cx.bank_free[bk] = [copy_evac(dT[:, uu * nfb + fb, tb * 512:(tb + 1) * 512], psum[:, bk, :], pt)]
                    def rhsA(kc, tb):
                        return A[:, kc, tb * 512:(tb + 1) * 512]
                    def ev_z(m, tb, bi, tok, uu=uu, nfb=nfb):
                        return [E.op("scalar", lambda s: s.activation(out=szg[:, uu * nfb + m, tb * 512:(tb + 1) * 512],
                                                                       in_=psum[:, bi, :], func=AF.Silu), [tok])]
                    gemm_B(win, j * D, D + f0, UW, KC, rhsA, ev_z)
                pm_free[0] = E.total("tensor")
                d_done = [E.total("scalar"), E.total("vector")]
                for uu in range(NUG):
                    nfb = UW // 128
                    def rhsD(kc, tb):
                        return dT[:, kc, tb * 512:(tb + 1) * 512]
                    def ev_y(m, tb, bi, tok, uu=uu, nfb=nfb):
                        fblk = g * NBG + uu * nfb + m
                        return [E.op("vector", lambda v: v.scalar_tensor_tensor(
                            gstp[:, m, tb * 512:(tb + 1) * 512], psum[:, bi, :], psfm[:, j, fblk:fblk + 1],
                            szg[:, uu * nfb + m, tb * 512:(tb + 1) * 512], ALU.mult, ALU.mult), [tok, gst_free[0]] + d_done)]
                    cx.bank_free[0] = cx.bank_free[0] + d_done
                    gemm_B(w_grp_d, (j * 4 + g) * Dg, uu * UW, UW, NBG, rhsD, ev_y)
                    nblk = UW // 128
                    b0 = g * NBG + uu * nfb
                    gst_free[0] = E.dma("sync", gts_d.rearrange("p (a b) -> p a b", a=KC)[:, b0:b0 + nblk, :], gstp[:, 0:nblk, :], "gst",
                                        waits=[E.total("vector")])
            return [E.total("gst")]

        def final_phase(r, ydst):
            xsrc = xs_d[r]
            fl = E.dma("sync", fgbc, fgbc_d, "ld0", waits=[E.total("vector"), E.total("gpsimd"), E.total("scalar")])
            xfree = [None, None]
            for t in range(NT):
                b = t % 2
                lt = E.dma("sync", xt[b], xsrc[t * 128:(t + 1) * 128, :], f"xt{b}",
                           waits=[xfree[b]] + ([E.total("xst0"), E.total("xst1"), E.total("gpsimd")] if t < 2 else []))
                a1 = E.op("scalar", lambda s: s.activation(out=junk, in_=xt[b], func=AF.Square, accum_out=ss[:, t:t + 1]), [lt])
                a2 = E.op("scalar", lambda s: s.activation(out=sd[:, t:t + 1], in_=ss[:, t:t + 1], func=AF.Sqrt,
                                                            bias=EPS_AP[0], scale=1.0 / D), [a1, eps_tok])
                v1 = E.op("vector", lambda v: v.reciprocal(rstd[:, t:t + 1], sd[:, t:t + 1]), [a2])
                v2 = E.op("vector", lambda v: v.scalar_tensor_tensor(xt[b], xt[b], rstd[:, t:t + 1], fgbc, ALU.mult, ALU.mult), [v1, fl])
                xfree[b] = E.dma("sync", ydst[t * 128:(t + 1) * 128, :], xt[b], f"fst{b}", waits=[v2])

        eps_tok = E.op("vector", lambda v: v.memset(EPS_AP[0], EPS), [])
        srcs = [xl_in, xc_in]
        for i in range(DEPTH):
            if i + 1 < DEPTH:
                ada_state["gen"] = ada_layer(i + 1)
            for r in range(2):
                xsrc = srcs[r] if i == 0 else xs_d[r]
                hready = norm_phase(i, r, xsrc)
                cx.bank_free[0] = cx.bank_free[0] + hready
                if DBG_STOP in (3, 31, 32, 33, 34, 35):
                    return finish()
                if i % 2 == 0:
                    gready = attn_phase(i, r, hready)
                    w2d, r0 = w_out_attn_d, (i // 2) * D
                else:
                    gready = pool_phase(i, r, hready)
                    w2d, r0 = w_out_pool_d, (i // 2) * D
                if DBG_STOP == 4:
                    return finish()
                outproj_phase(i, r, w2d, r0, xsrc, gready + [E.total("tensor")])
                if DBG_STOP == 5:
                    return finish()
                if DBG_STOP == 6 and r == 1:
                    return finish()
                if DBG_STOP == 7 and r == 1 and i == 1:
                    return finish()
            ada_flush()
        for r in range(2):
            final_phase(r, [yl_d, yc_d][r])
        finish()

    EPS_AP = [sb("epsc", [128, 1], F32)[:]]
    TILE_LIST = {}
    cx0 = Cx(nc, sems, None, None)
    program(cx0)
    with nc.Block() as block:
        for eng in ENGS:
            def body(h, eng=eng):
                cx = Cx(nc, sems, eng, h)
                cx.tile_src = TILE_LIST
                program(cx)
            getattr(block, eng)(body)
    es.close()
    return nc


_NC_CACHE = {}


def _prep(inputs, D):
    f = lambda a: np.ascontiguousarray(np.asarray(a, dtype=np.float32))
    KC = D // 128
    x_prompt, x_sample = f(inputs["x_prompt"]), f(inputs["x_sample"])
    c, c_ctx = f(inputs["c"]), f(inputs["c_ctx"])
    cache_k, cache_v = f(inputs["cache_k"]), f(inputs["cache_v"])
    norm_g, b_ada = f(inputs["norm_g"]), f(inputs["b_ada"])
    m01, pml, pmc = _static_consts()
    fm = lambda v: np.ascontiguousarray(v.reshape(v.shape[0], -1, 128).transpose(2, 0, 1).reshape(128, -1))
    shared = {
        "ngfm": fm(norm_g),
        "fgbc": np.ascontiguousarray(np.broadcast_to(f(inputs["final_norm_g"])[None, :], (128, D))),
        "badafm": fm(b_ada),
        "badag": np.ascontiguousarray(np.broadcast_to(b_ada[:, 2 * D:].reshape(1, 4 * D), (2, 4 * D))),
        "psfm": fm(f(inputs["pool_scale"])),
        "rpbx": _rpb_expand(f(inputs["rpb"])).reshape(-1, NJ * 64),
        "identc": np.eye(128, dtype=np.float32),
        "onesc": np.ones((128, 128), np.float32),
        "pmlc": pml.reshape(-1, 512), "pmcc": pmc.reshape(-1, 256),
        "w_grp_pool": f(inputs["w_grp_pool"]).reshape(-1, D // 4),
    }
    BR = min(8, KC) * 128
    for n, cols in (("w_ada", 3 * D), ("w_in_attn", 4 * D), ("w_out_attn", D), ("w_in_pool", 2 * D), ("w_out_pool", D)):
        w = f(inputs[n]).reshape(-1, cols)
        for b in range(w.shape[0] // BR):
            shared[f"{n}_{b}"] = w[b * BR:(b + 1) * BR]
    maps = []
    for b in range(8):
        cond = np.stack([c[b], c_ctx], 0)
        m = dict(shared)
        m["xl"] = x_sample[b]
        m["xc"] = x_prompt[4 * b:4 * b + 4].reshape(T, D)
        m["condT"] = np.ascontiguousarray(cond.reshape(2, KC, 128).transpose(2, 1, 0).reshape(128, KC * 2))
        m["ck"] = cache_k[b].reshape(2 * PAST, D)
        m["cv"] = cache_v[b].reshape(2 * PAST, D)
        maps.append(m)
    return maps


def _run(inputs, D):
    if D not in _NC_CACHE:
        _NC_CACHE[D] = build(D)
    nc = _NC_CACHE[D]
    maps = _prep(inputs, D)
    res = run_bass_kernel_spmd(nc, maps, core_ids=list(range(8)))
    r = res.results
    y_sample = np.stack([r[b]["yl"] for b in range(8)], 0)
    y_prompt = np.concatenate([r[b]["yc"].reshape(4, 256, D) for b in range(8)], 0)
    H = D // 128
    nk = np.concatenate([r[b]["nk"].reshape(4, 2, 256, H, 128) for b in range(8)], 0)
    nv = np.concatenate([r[b]["nv"].reshape(4, 2, 256, H, 128) for b in range(8)], 0)
    return (y_prompt.astype(np.float32), y_sample.astype(np.float32), nk.astype(np.float32), nv.astype(np.float32))


def kernel(**inputs):
    return _run(inputs, 4096)
```

```python
import numpy as np
import concourse.bass as bass
import concourse.mybir as mybir
from concourse.bass_utils import run_bass_kernel_spmd

F32, BF16 = mybir.dt.float32, mybir.dt.bfloat16
AF = mybir.ActivationFunctionType
ALU = mybir.AluOpType

T = 1024
NT = 8
GRID_W, ROWS, WIN_H, WIN_W = 64, 16, 8, 16
PAST = 512
DEPTH = 4
EPS = 1e-6
NEG = -1e30
POOL_WINDOWS = (2, 4, 8, 16)
NS = 4
NJ = 22
ENGS = ("sync", "scalar", "vector", "gpsimd", "tensor")
DBG_STOP = 0


def _win_tiles():
    return [[0, 1, 2, 3, 4, 5], [2, 3, 4, 5, 6, 7]]


def _static_consts():
    r = np.arange(ROWS)
    rs = np.clip(r - WIN_H // 2, 0, ROWS - WIN_H)
    col = np.arange(GRID_W)
    cs = np.clip(col - WIN_W // 2, 0, GRID_W - WIN_W)
    col_in = (col[None, :] >= cs[:, None]) & (col[None, :] < cs[:, None] + WIN_W)
    m01 = np.zeros((12, 128, 512), np.float32)
    for qc in range(2):
        for ti, kc in enumerate(_win_tiles()[qc]):
            for krl in range(2):
                kr = 2 * kc + krl
                for qi in range(8):
                    qr = qc * 8 + qi
                    if rs[qr] <= kr < rs[qr] + WIN_H:
                        m01[qc * 6 + ti, krl * 64:(krl + 1) * 64, qi * 64:(qi + 1) * 64] = col_in.T
    def pm(w, L):
        t = np.arange(L)
        lo = np.clip(t - w // 2, 0, L)
        hi = np.clip(t + w // 2, 0, L)
        tp = np.arange(L)[:, None]
        m = ((tp >= lo[None, :]) & (tp < hi[None, :])).astype(np.float64) / (hi - lo)[None, :]
        m -= np.eye(L)
        return m.astype(np.float32)
    pml = np.zeros((4, 10, 128, 512), np.float32)
    pmc = np.zeros((4, 2, 128, 256), np.float32)
    for g, w in enumerate(POOL_WINDOWS):
        ml = pm(w, 1024)
        mc = pm(w, 256)
        for tb in range(2):
            for i, kb in enumerate(_pml_kbs(tb)):
                pml[g, tb * 5 + i] = ml[kb * 128:(kb + 1) * 128, tb * 512:(tb + 1) * 512]
        for kb in range(2):
            pmc[g, kb] = mc[kb * 128:(kb + 1) * 128, :]
    return m01, pml, pmc


def _pml_kbs(tb):
    return [0, 1, 2, 3, 4] if tb == 0 else [3, 4, 5, 6, 7]


def _rpb_expand(rpb):
    NL, H = rpb.shape[:2]
    krl = np.arange(2)[:, None, None, None]
    kcol = np.arange(64)[None, :, None, None]
    jj = np.arange(NJ)[None, None, :, None]
    qcol = np.arange(64)[None, None, None, :]
    drt = 10 + krl - jj
    cs = np.clip(qcol - WIN_W // 2, 0, GRID_W - WIN_W)
    colok = (kcol >= cs) & (kcol < cs + WIN_W)
    dri = np.clip(drt + WIN_H - 1, 0, 2 * WIN_H - 2)
    dci = np.clip(kcol - qcol, -(WIN_W - 1), WIN_W - 1) + WIN_W - 1
    shape = (2, 64, NJ, 64)
    dri, dci = np.broadcast_to(dri, shape), np.broadcast_to(dci, shape)
    g = rpb[:, :, dri, dci]
    outs = []
    for lo, hi in ((-(WIN_H // 2), WIN_H // 2 - 1), (-(WIN_H - 1), WIN_H - 1)):
        valid = np.broadcast_to((drt >= lo) & (drt <= hi) & colok, shape)
        outs.append(np.where(valid[None, None], g, np.float32(NEG)).astype(np.float32))
    out = np.stack(outs, 2)
    return np.ascontiguousarray(out.reshape(NL, H, 2, 128, NJ * 64))


def _win_variant(qc, kc):
    return 1 if (qc, kc) in ((0, 2), (0, 3), (1, 4), (1, 5)) else 0


class Cx:
    def __init__(self, nc, sems, cur, handle):
        self.nc, self.sems, self.cur, self.h = nc, sems, cur, handle
        self.cnt = {}
        self.waited = {}
        self.tile_idx = 0
        self.bank_free = [[] for _ in range(8)]
        self.slot_free = {}
        self.tile_src = {}

    def _wait(self, tok):
        if tok is None:
            return
        s, v = tok
        if self.waited.get(s, 0) >= v:
            return
        self.waited[s] = v
        self.h.wait_ge(self.sems[s], v)

    def op(self, eng, fn, waits=(), sig=True):
        tok = None
        if sig:
            self.cnt[eng] = self.cnt.get(eng, 0) + 1
            tok = (eng, self.cnt[eng])
        if eng == self.cur:
            for w in waits:
                self._wait(w)
            inst = fn(self.h)
            if sig:
                inst.then_inc(self.sems[eng], 1)
        return tok

    STORE_SEMS = ("ost", "gst", "xst0", "xst1", "fst0", "fst1", "gss")

    def dma(self, eng, out, in_, sem, waits=()):
        if eng == "sync" and sem not in self.STORE_SEMS:
            waits = list(waits) + [self.total(x) for x in self.STORE_SEMS]
        self.cnt[sem] = self.cnt.get(sem, 0) + 16
        tok = (sem, self.cnt[sem])
        if eng == self.cur:
            for w in waits:
                self._wait(w)
            self.h.dma_start(out=out, in_=in_).then_inc(self.sems[sem], 16)
        return tok

    def total(self, sem):
        c = self.cnt.get(sem, 0)
        return (sem, c) if c else None


SEM_NAMES = (list(ENGS) + [f"ring{i}" for i in range(NS)] +
             ["xt0", "xt1", "xst0", "xst1", "ld0", "ld1", "ldc", "cst", "bh0", "bh1", "gst", "ost", "fst0", "fst1", "misc", "ldo", "ldm", "ldi", "bhs", "gss"])


def build(D):
    KC = D // 128
    H = KC
    NU = D // 512
    KT = min(8, KC)
    Dg = D // 4
    NBG = Dg // 128
    UW = min(512, Dg)
    NUG = Dg // UW
    NHG = H // 4
    SCALE = 128.0 ** -0.5

    nc = bass.Bass("TRN2", target_bir_lowering=False)
    dt = lambda n, s, k="ExternalInput", d=F32: nc.dram_tensor(n, list(s), d, kind=k)
    xl_in = dt("xl", [T, D]).ap()
    xc_in = dt("xc", [T, D]).ap()
    condT_d = dt("condT", [128, KC * 2]).ap()
    ck_d = dt("ck", [2 * PAST, D]).ap()
    cv_d = dt("cv", [2 * PAST, D]).ap()
    ngfm_d = dt("ngfm", [128, 4 * KC]).ap()
    fgbc_d = dt("fgbc", [128, D]).ap()
    badafm_d = dt("badafm", [128, 4 * 3 * KC]).ap()
    badag_d = dt("badag", [2, 4 * D]).ap()
    psfm_d = dt("psfm", [128, 2 * KC]).ap()
    rpbx_d = dt("rpbx", [2 * H * 2 * 128, NJ * 64]).ap()
    ident_d = dt("identc", [128, 128]).ap()
    ones_d = dt("onesc", [128, 128]).ap()
    pml_d = dt("pmlc", [40 * 128, 512]).ap()
    pmc_d = dt("pmcc", [8 * 128, 256]).ap()
    BR = KT * 128
    blocks = lambda n, rows, cols: [dt(f"{n}_{b}", [BR, cols]).ap() for b in range(rows // BR)]
    w_ada_d = blocks("w_ada", 4 * D, 3 * D)
    w_in_attn_d = blocks("w_in_attn", 2 * D, 4 * D)
    w_out_attn_d = blocks("w_out_attn", 2 * D, D)
    w_in_pool_d = blocks("w_in_pool", 2 * D, 2 * D)
    w_grp_d = dt("w_grp_pool", [2 * 4 * Dg, Dg]).ap()
    w_out_pool_d = blocks("w_out_pool", 2 * D, D)
    yl_d = dt("yl", [T, D], "ExternalOutput").ap()
    yc_d = dt("yc", [T, D], "ExternalOutput").ap()
    nk_d = dt("nk", [4 * 2 * 256, D], "ExternalOutput").ap()
    nv_d = dt("nv", [4 * 2 * 256, D], "ExternalOutput").ap()
    xs_d = [dt("xls", [T, D], "Internal").ap(), dt("xcs", [T, D], "Internal").ap()]
    gts_d = dt("gts", [128, KC * T], "Internal", BF16).ap()
    gate_s = dt("gates", [4 * 2, D], "Internal").ap()

    ARENA = 96 * 1024
    from contextlib import ExitStack
    es = ExitStack()
    sb = lambda n, s, d: es.enter_context(nc.sbuf_tensor(n, list(s), d))
    A = sb("A", [128, KC, T], BF16)
    ring = sb("ring", [128, NS, KT, 512], BF16)
    arena = sb("arena", [128, ARENA // 2], BF16)
    ident = sb("ident", [128, 128], F32)
    ones = sb("ones", [128, 128], BF16)
    identb = sb("identb", [128, 128], BF16)
    condT = sb("condTs", [128, KC * 2], F32)
    scT = sb("scT", [128, KC, 2], BF16)
    shT = sb("shT", [128, 4, 2, KC], F32)
    g1sT = sb("g1sT", [128, 4, 2, KC], F32)
    ngfm = sb("ngfms", [128, 4, KC], F32)
    badafm = sb("badafms", [128, 4, 3 * KC], F32)
    bp1 = sb("bp1", [128, 4, KC], F32)
    psfm = sb("psfms", [128, 2, KC], F32)
    gsm = sb("gsm", [2, 512], F32)
    ss = sb("ss", [128, 8], F32)
    sd = sb("sd", [128, 8], F32)
    rstd = sb("rstd", [128, 8], F32)
    psum = es.enter_context(nc.psum_tensor("psum", [128, 8, 512], F32))
    sems = {n: es.enter_context(nc.semaphore(n)) for n in SEM_NAMES}

    def carve(off, shape, dtype):
        n = int(np.prod(shape))
        if dtype == BF16:
            v = arena[:, off // 2: off // 2 + n]
        else:
            v = arena[:, off // 2: off // 2 + 2 * n].bitcast(F32)
        if len(shape) == 1:
            return v
        names = " ".join(f"d{i}" for i in range(len(shape)))
        return v.rearrange(f"p ({names}) -> p {names}", **{f"d{i}": shape[i] for i in range(1, len(shape))})

    K = 1024
    xt = [carve(0, [D], F32), carve(16 * K, [D], F32)]
    xeb = [carve(0, [8, 512], F32), carve(16 * K, [8, 512], F32)]
    junk = carve(32 * K, [D], BF16)
    fgbc = carve(40 * K, [D], F32)
    gatebc = carve(32 * K, [D], F32)
    tmpe = [carve(48 * K, [512], F32), carve(50 * K, [512], F32)]
    gbias = carve(52 * K, [D], F32)
    qT = carve(0, [4, T], BF16)
    kT = carve(8 * K, [4, T + PAST], BF16)
    szT = carve(20 * K, [4, T], BF16)
    vtm = carve(28 * K, [12, 512], BF16)
    gst = carve(40 * K, [4, T], BF16)
    stmp = [carve(48 * K, [512], F32), carve(50 * K, [512], F32)]
    ebuf = [carve(52 * K, [512], BF16), carve(53 * K, [512], BF16)]
    pbuf = [carve(54 * K, [512], BF16), carve(55 * K, [512], BF16), carve(56 * K, [512], BF16)]
    rden = carve(57 * K, [512], F32)
    t1 = carve(59 * K, [512], F32)
    cst = carve(61 * K, [4, 512], F32)
    bhs = carve(69 * K, [2, NJ, 64], F32)
    bhb = [carve(69 * K + 2 * NJ * 256, [2, NJ * 64], BF16), carve(69 * K + 3 * NJ * 256, [2, NJ * 64], BF16)]
    ost = carve(61 * K, [8, 512], F32)
    kTf = carve(77 * K, [4, T], F32)
    utm = carve(0, [8, UW], BF16)
    dT = carve(8 * K, [NBG, T], BF16)
    szg = carve(24 * K, [NBG, T], BF16)
    gstp = carve(40 * K, [4, T], BF16)
    pml = carve(48 * K, [10, 512], BF16)
    pmc = carve(58 * K, [2, 256], BF16)
    gsb = carve(0, [D], F32)
    bg2 = carve(16 * K, [D], F32)

    def wview(w2d):
        return w2d.rearrange("(kc p) n -> p kc n", p=128)

    def program(cx):
        E = cx
        bank = lambda b: psum[:, b, :]

        def issue_tile(k):
            if k not in cx.tile_src:
                return
            src, nk, ncols = cx.tile_src[k]
            slot = k % NS
            E.dma("gpsimd", ring[:, slot, 0:nk, 0:ncols], src, f"ring{slot}",
                  waits=[cx.slot_free.get(k - NS)])

        def tile_tok(k):
            return (f"ring{k % NS}", 16 * (k // NS + 1))

        def gemm(w2d, r0, c0, ncols, nkc, naccs, mm_fn, evac_fn):
            blocked = isinstance(w2d, list)
            wv = None if blocked else wview(w2d)
            kc0 = r0 // 128
            ntile = (nkc + KT - 1) // KT
            acc_tok = [None] * naccs
            for kt in range(ntile):
                nk = min(KT, nkc - kt * KT)
                n = cx.tile_idx
                cx.tile_idx += 1
                if cx.cur is None:
                    if blocked:
                        src = wview(w2d[r0 // BR + kt])[:, 0:nk, c0:c0 + ncols]
                    else:
                        src = wv[:, kc0 + kt * KT: kc0 + kt * KT + nk, c0:c0 + ncols]
                    TILE_LIST[n] = (src, nk, ncols)
                if n == 0:
                    for k in range(NS):
                        issue_tile(k)
                slot = n % NS
                ttok = None
                for k in range(nk):
                    kc = kt * KT + k
                    for bi in range(naccs):
                        waits = []
                        if k == 0 and bi == 0:
                            waits.append(tile_tok(n))
                        if kc == 0:
                            waits += cx.bank_free[bi]
                            cx.bank_free[bi] = []
                        last_tile = (k == nk - 1 and bi == naccs - 1)
                        last_kc = (kc == nkc - 1)
                        tok = E.op("tensor",
                                   lambda t, bi=bi, kc=kc, k=k, slot=slot, last_kc=last_kc:
                                   mm_fn(t, bi, ring[:, slot, k, :], kc, kc == 0, last_kc),
                                   waits, sig=(last_tile or last_kc))
                        if last_kc:
                            acc_tok[bi] = tok
                        if last_tile:
                            ttok = tok
                cx.slot_free[n] = ttok
                issue_tile(n + NS)
            for bi in range(naccs):
                cx.bank_free[bi] = evac_fn(bi, acc_tok[bi])
            ada_tick()

        def gemm_B(w2d, r0, c0, ncols, nkc, rhs_fn, evac_fn, ntb=2, nfree=512):
            nm = ncols // 128
            def mm(t, bi, wt, kc, st, sp):
                m, tb = bi // ntb, bi % ntb
                return t.matmul(psum[:, bi, 0:nfree], wt[:, m * 128:(m + 1) * 128], rhs_fn(kc, tb), start=st, stop=sp)
            gemm(w2d, r0, c0, ncols, nkc, nm * ntb, mm, lambda bi, tok: evac_fn(bi // ntb, bi % ntb, bi, tok))

        def gemm_A(w2d, r0, c0, ncols, nkc, lhs_fn, evac_fn):
            def mm(t, bi, wt, kc, st, sp):
                return t.matmul(psum[:, bi, 0:ncols], lhs_fn(kc, bi), wt[:, 0:ncols], start=st, stop=sp)
            gemm(w2d, r0, c0, ncols, nkc, NT, mm, evac_fn)

        _alt = [0]
        def copy_evac(out, in_, tok, scale=None, extra=()):
            _alt[0] ^= 1
            if _alt[0]:
                if scale is None:
                    return E.op("vector", lambda v: v.tensor_copy(out, in_), [tok, *extra])
                return E.op("vector", lambda v: v.tensor_scalar(out, in_, scale, None, ALU.mult), [tok, *extra])
            return E.op("scalar", lambda s: s.activation(out=out, in_=in_, func=AF.Copy,
                                                          scale=(1.0 if scale is None else scale)), [tok, *extra])

        c_tok = []
        for o, i_ in ((ident[:], ident_d), (condT[:], condT_d), (ngfm[:].rearrange("p a b -> p (a b)"), ngfm_d),
                      (badafm[:].rearrange("p a b -> p (a b)"), badafm_d), (psfm[:].rearrange("p a b -> p (a b)"), psfm_d)):
            c_tok.append(E.dma("sync", o, i_, "misc"))
        c_all = E.total("misc")
        ones_tok = E.dma("gpsimd", ones[:], ones_d, "ldo")
        identb_tok = E.dma("gpsimd", identb[:], ident_d, "ldi")
        t_sc = E.op("scalar", lambda s: s.activation(out=scT[:].rearrange("p a b -> p (a b)"), in_=condT[:], func=AF.Silu), [c_all])
        t_bp1 = E.op("vector", lambda v: v.tensor_scalar(bp1[:], badafm[:, :, KC:2 * KC], 1.0, None, ALU.add), [c_all])

        def finish():
            if cx.cur == "sync":
                for s_ in Cx.STORE_SEMS:
                    cx._wait(E.total(s_))

        if DBG_STOP == 1:
            return finish()
        gsm_free = [None]
        cx.bank_free[0] = [t_sc]

        def ada_layer(i):
            wa = w_ada_d
            for u in range(2 * NU):
                def rhs_fn(kc, tb):
                    return scT[:, kc, :]
                def evac(m, tb, bi, tok, u=u, i=i):
                    blk = u * 4 + m
                    if blk < KC:
                        return [E.op("vector", lambda v: v.tensor_scalar(
                            shT[:, i, :, blk], psum[:, bi, 0:2], badafm[:, i, blk:blk + 1], None, ALU.add), [tok, c_all])]
                    c = blk - KC
                    return [E.op("vector", lambda v: v.tensor_scalar(
                        g1sT[:, i, :, c], psum[:, bi, 0:2], bp1[:, i, c:c + 1], ngfm[:, i, c:c + 1], ALU.add, ALU.mult),
                        [tok, t_bp1, c_all])]
                gemm_B(wa, i * D, u * 512, 512, KC, rhs_fn, evac, ntb=1, nfree=2)
                yield
            for u in range(NU):
                def mm(t, bi, wt, kc, st, sp):
                    return t.matmul(psum[0:2, bi, :], scT[:, kc, :], wt[:, 0:512], start=st, stop=sp)
                def evac(bi, tok, u=u):
                    tk = E.op("vector", lambda v: v.tensor_copy(gsm[0:2, :], psum[0:2, bi, :]), [tok, gsm_free[0]])
                    gsm_free[0] = E.dma("sync", gate_s[2 * i:2 * i + 2, u * 512:(u + 1) * 512], gsm[0:2, :], "gss", waits=[tk])
                    return [tk]
                gemm(wa, i * D, 2 * D + u * 512, 512, KC, 1, mm, evac)
                yield

        ada_state = {"gen": None, "n": 0, "busy": False}

        def ada_tick():
            if ada_state["busy"] or ada_state["gen"] is None:
                return
            ada_state["n"] += 1
            if ada_state["n"] % 2:
                return
            ada_state["busy"] = True
            if next(ada_state["gen"], "end") == "end":
                ada_state["gen"] = None
            ada_state["busy"] = False

        def ada_flush():
            ada_state["busy"] = True
            if ada_state["gen"] is not None:
                for _ in ada_state["gen"]:
                    pass
            ada_state["gen"] = None
            ada_state["busy"] = False

        ada_state["gen"] = ada_layer(0)
        ada_flush()
        mod_ready = [E.total("vector"), E.total("gst")]
        if DBG_STOP == 2:
            return finish()


        def norm_phase(i, r, xsrc):
            xfree = [None, None]
            v2s = {}

            def stage_a(t):
                b = t % 2
                lt = E.dma("sync", xt[b], xsrc[t * 128:(t + 1) * 128, :], f"xt{b}",
                           waits=[xfree[b]] + (([E.total("vector"), E.total("gst"), E.total("xst0"), E.total("xst1"), E.total("gpsimd")]) if t < 2 else []))
                a1 = E.op("scalar", lambda s: s.activation(out=junk, in_=xt[b], func=AF.Square,
                                                            accum_out=ss[:, t:t + 1]), [lt])
                a2 = E.op("scalar", lambda s: s.activation(out=sd[:, t:t + 1], in_=ss[:, t:t + 1], func=AF.Sqrt,
                                                            bias=EPS_AP[0], scale=1.0 / D), [a1, eps_tok])
                v1 = E.op("vector", lambda v: v.reciprocal(rstd[:, t:t + 1], sd[:, t:t + 1]), [a2])
                v2s[t] = E.op("vector", lambda v: v.tensor_scalar(xt[b], xt[b], rstd[:, t:t + 1], None, ALU.mult), [v1])

            def stage_b(t):
                b = t % 2
                v2 = v2s[t]
                for c4 in range(KC // 4):
                    bk = (t * (KC // 4) + c4) % 8
                    waits = [v2, c_all] + cx.bank_free[bk]
                    cx.bank_free[bk] = []
                    for q in range(4):
                        c = c4 * 4 + q
                        pt = E.op("tensor", lambda tt, c=c, q=q, bk=bk: tt.transpose(
                            psum[:, bk, q * 128:(q + 1) * 128], xt[b][:, c * 128:(c + 1) * 128], ident[:]),
                            waits if q == 0 else [], sig=(q == 3))
                    fr = []
                    for q in range(4):
                        c = c4 * 4 + q
                        o = A[:, c, t * 128:(t + 1) * 128]
                        i_ = psum[:, bk, q * 128:(q + 1) * 128]
                        if bk % 2 == 0:
                            fr.append(E.op("scalar", lambda s, o=o, i_=i_, c=c: s.activation(
                                out=o, in_=i_, func=AF.Identity, scale=g1sT[:, i, r, c:c + 1], bias=shT[:, i, r, c:c + 1]), [pt]))
                        else:
                            fr.append(E.op("vector", lambda v, o=o, i_=i_, c=c: v.tensor_scalar(
                                o, i_, g1sT[:, i, r, c:c + 1], shT[:, i, r, c:c + 1], ALU.mult, ALU.add), [pt]))
                    cx.bank_free[bk] = fr
                xfree[b] = pt

            stage_a(0)
            for t in range(NT):
                if t + 1 < NT:
                    stage_a(t + 1)
                stage_b(t)
            return [E.total("scalar"), E.total("vector")]

        def outproj_phase(i, r, w2d, r0, xsrc, gts_ready):
            xdst = xs_d[r]
            lt = E.dma("sync", A[:].rearrange("p a b -> p (a b)"), gts_d, "ld0", waits=gts_ready)
            gl0 = E.dma("sync", gatebc, bass.AP(gate_s.tensor, (2 * i + r) * D, [[0, 128], [1, D]]), "ld1",
                        waits=[E.total("vector"), E.total("gpsimd")])
            gb = E.dma("sync", gbias, bass.AP(badag_d.tensor, i * D, [[0, 128], [1, D]]), "ldm")
            gl = E.op("gpsimd", lambda g: g.tensor_tensor(gatebc, gatebc, gbias, ALU.add), [gl0, gb])
            cx.bank_free[0] = cx.bank_free[0] + [lt]
            xe_free = [None, None]
            xl_tok = {}
            def load_xe(cu):
                b = cu % 2
                xl_tok[cu] = E.dma("sync", xeb[b],
                                   xsrc.rearrange("(t p) n -> p t n", p=128)[:, :, cu * 512:(cu + 1) * 512],
                                   f"xt{b}", waits=[xe_free[b]])
            load_xe(0)
            st_toks = [None, None]
            fin = []
            for cu in range(NU):
                b = cu % 2
                if cu + 1 < NU:
                    load_xe(cu + 1)
                xe = xeb[b]
                def lhs_fn(kc, t):
                    return A[:, kc, t * 128:(t + 1) * 128]
                def evac(bi, tok, cu=cu, xe=xe):
                    tb_ = bi % 2
                    d1 = E.op("vector", lambda v: v.tensor_tensor(tmpe[tb_], psum[:, bi, :], gatebc[:, cu * 512:(cu + 1) * 512], ALU.mult),
                              [tok, gl, lt] + fin[-2:-1])
                    p1 = E.op("gpsimd", lambda g: g.tensor_tensor(xe[:, bi, :], tmpe[tb_], xe[:, bi, :], ALU.add), [d1, xl_tok[cu]])
                    fin.append(p1)
                    return [d1]
                gemm_A(w2d, r0, cu * 512, 512, KC, lhs_fn, evac)
                st = E.dma("sync", xdst.rearrange("(t p) n -> p t n", p=128)[:, :, cu * 512:(cu + 1) * 512], xe,
                           f"xst{b}", waits=[fin[-1]])
                xe_free[b] = st
            return [E.total("xst0"), E.total("xst1")]

        gst_free = [None]
        ost_free = [None]
        bh_free = [None, None]
        bhs_free = [None]

        def attn_phase(i, r, hready):
            j = i // 2
            win = w_in_attn_d
            cvt_prev = [None]
            bh_tok = {}
            for hg in range(NHG):
                if r == 0:
                    lk = E.dma("sync", cst, ck_d[j * PAST:(j + 1) * PAST, hg * 512:(hg + 1) * 512].rearrange("(c p) n -> p c n", p=128),
                               "cst", waits=[cvt_prev[0], E.total("tensor")] if hg == 0 else [cvt_prev[0]])
                def rhsA(kc, tb):
                    return A[:, kc, tb * 512:(tb + 1) * 512]
                def ev_q(m, tb, bi, tok):
                    return [copy_evac(qT[:, m, tb * 512:(tb + 1) * 512], psum[:, bi, :], tok, scale=SCALE)]
                def ev_k(m, tb, bi, tok):
                    if r == 1:
                        o16, o32, i_ = kT[:, m, tb * 512:(tb + 1) * 512], kTf[:, m, tb * 512:(tb + 1) * 512], psum[:, bi, :]
                        if bi % 2 == 0:
                            return [E.op("scalar", lambda s: s.activation(out=o16, in_=i_, func=AF.Copy), [tok]),
                                    E.op("scalar", lambda s: s.activation(out=o32, in_=i_, func=AF.Copy), [tok])]
                        return [E.op("vector", lambda v: v.tensor_copy(o16, i_), [tok]),
                                E.op("vector", lambda v: v.tensor_copy(o32, i_), [tok])]
                    return [copy_evac(kT[:, m, tb * 512:(tb + 1) * 512], psum[:, bi, :], tok)]
                def ev_z(m, tb, bi, tok):
                    return [E.op("scalar", lambda s: s.activation(out=szT[:, m, tb * 512:(tb + 1) * 512], in_=psum[:, bi, :], func=AF.Silu), [tok])]
                def lhsA(kc, t):
                    return A[:, kc, t * 128:(t + 1) * 128]
                def ev_v(bi, tok):
                    if r == 1:
                        return [E.op("vector", lambda v: v.tensor_copy(vtm[:, bi, :], psum[:, bi, :]), [tok]),
                                E.op("vector", lambda v: v.tensor_copy(ost[:, bi, :], psum[:, bi, :]), [tok, ost_free[0]])]
                    return [copy_evac(vtm[:, bi, :], psum[:, bi, :], tok)]
                def store_ost(dst):
                    toks = []
                    for s in range(4):
                        toks.append(E.dma("sync", dst[(s * 2 + j) * 256:(s * 2 + j + 1) * 256, hg * 512:(hg + 1) * 512]
                                          .rearrange("(t p) n -> p t n", p=128), ost[:, 2 * s:2 * s + 2, :], "ost",
                                          waits=[E.total("vector"), E.total("scalar")]))
                    ost_free[0] = toks[-1]
                    return toks
                gemm_B(win, j * D, hg * 512, 512, KC, rhsA, ev_q)
                gemm_B(win, j * D, D + hg * 512, 512, KC, rhsA, ev_k)
                k_done = [E.total("scalar"), E.total("vector")]
                gemm_B(win, j * D, 3 * D + hg * 512, 512, KC, rhsA, ev_z)
                gemm_A(win, j * D, 2 * D + hg * 512, 512, KC, lhsA, ev_v)
                if r == 1:
                    store_ost(nv_d)
                    def ev_kt(bi, tok):
                        return [E.op("vector", lambda v: v.tensor_copy(ost[:, bi, :], psum[:, bi, :]), [tok, ost_free[0]])]
                    for t in range(NT):
                        w_ = k_done + cx.bank_free[t]
                        cx.bank_free[t] = []
                        for m in range(4):
                            pt = E.op("tensor", lambda tt, m=m, t=t: tt.transpose(
                                psum[:, t, m * 128:(m + 1) * 128], kTf[:, m, t * 128:(t + 1) * 128], ident[:]),
                                w_ if m == 0 else [], sig=(m == 3))
                        cx.bank_free[t] = ev_kt(t, pt)
                    store_ost(nk_d)
                proj_done = [E.total("scalar"), E.total("vector")]
                jobs = []
                if r == 0:
                    for m in range(4):
                        bk = 7
                        w_ = [lk] + cx.bank_free[bk]
                        cx.bank_free[bk] = []
                        for c in range(4):
                            pt = E.op("tensor", lambda tt, c=c, m=m: tt.transpose(psum[:, 7, c * 128:(c + 1) * 128],
                                                                                  cst[:, c, m * 128:(m + 1) * 128], ident[:]),
                                      w_ if c == 0 else [], sig=(c == 3))
                        cx.bank_free[bk] = [copy_evac(kT[:, m, T:T + PAST], psum[:, 7, :], pt)]
                    kc_done = [E.total("scalar"), E.total("vector")]
                    lv = E.dma("sync", cst, cv_d[j * PAST:(j + 1) * PAST, hg * 512:(hg + 1) * 512].rearrange("(c p) n -> p c n", p=128),
                               "cst", waits=kc_done + [E.total("tensor")])
                    cvt = E.op("vector", lambda v: v.tensor_copy(vtm[:, 8:12, :], cst), [lv])
                    cvt_prev[0] = cvt
                    for m in range(4):
                        for qc in range(2):
                            for ti, kc in enumerate(_win_tiles()[qc]):
                                jobs.append(dict(m=m, g=(m, qc), kc=kc, q0=qc * 512, nq=512, win=qc * 6 + ti,
                                                 jj0=10 - 2 * kc + 8 * qc, ko=kc * 128, var=_win_variant(qc, kc)))
                            for c in range(4):
                                jobs.append(dict(m=m, g=(m, qc), kc=8 + c, q0=qc * 512, nq=512, win=None, ko=T + c * 128))
                else:
                    cvt = None
                    for m in range(4):
                        for s in range(4):
                            for c in range(2):
                                jobs.append(dict(m=m, g=(m, s), kc=2 * s + c, q0=s * 256, nq=256, win=None, ko=(2 * s + c) * 128))
                PV_tok, last_bh = {}, {}
                fin_last = [None]
                def load_bh(hh):
                    if hh in bh_tok or hh >= H:
                        return
                    r0_ = (j * H + hh) * 2 * 128
                    ld = E.dma("sync", bhs, rpbx_d[r0_:r0_ + 256, :].rearrange("(v p) (a b) -> p v a b", p=128, a=NJ), "bhs",
                               waits=[bhs_free[0]])
                    bh_tok[hh] = E.op("gpsimd", lambda g: g.tensor_copy(bhb[hh % 2], bhs.rearrange("p v a b -> p v (a b)")),
                                      [ld, bh_free[hh % 2]])
                    bhs_free[0] = bh_tok[hh]
                if r == 0:
                    load_bh(hg * 4)
                nj = len(jobs)
                S_tok = [None] * nj
                P_tok = [None] * nj
                grp_idx = {}
                for jb in jobs:
                    grp_idx.setdefault(jb["g"], len(grp_idx))
                def emit_S(x):
                    jb = jobs[x]
                    bk = x % 3
                    w_ = proj_done + cx.bank_free[bk] + ([cvt] if cvt else []) + ([E.total("scalar"), E.total("vector")] if x == 0 else [])
                    cx.bank_free[bk] = []
                    iswin = jb["win"] is not None
                    S_tok[x] = E.op("tensor", lambda tt: tt.matmul(psum[:, bk, 0:jb["nq"]], kT[:, jb["m"], jb["ko"]:jb["ko"] + 128],
                                                                   qT[:, jb["m"], jb["q0"]:jb["q0"] + jb["nq"]], start=True, stop=not iswin),
                                    w_, sig=not iswin)
                    if iswin:
                        m = jb["m"]
                        hh = hg * 4 + m
                        load_bh(hh)
                        load_bh(hh + 1)
                        c0_ = jb["jj0"] * 64
                        S_tok[x] = E.op("tensor", lambda tt: tt.matmul(psum[:, bk, :], identb[:], bhb[m % 2][:, jb["var"], c0_:c0_ + 512],
                                                                       start=False, stop=True), [bh_tok[hh], identb_tok])
                        bh_free[m % 2] = S_tok[x]
                def emit_elem(x):
                    jb = jobs[x]
                    bk = x % 3
                    nq = jb["nq"]
                    pb = pbuf[x % 3]
                    pfree = PV_tok.get(x - 3)
                    a1 = E.op("scalar", lambda s: s.activation(out=pb[:, 0:nq], in_=psum[:, bk, 0:nq], func=AF.Exp), [S_tok[x], pfree])
                    P_tok[x] = a1
                    cx.bank_free[bk] = [a1]
                def emit_PV(x):
                    jb = jobs[x]
                    gi = grp_idx[jb["g"]]
                    nq = jb["nq"]
                    ob, db = 3 + gi % 2, 5 + gi % 2
                    first = (x == 0 or jobs[x - 1]["g"] != jb["g"])
                    lastj = (x == nj - 1 or jobs[x + 1]["g"] != jb["g"])
                    w_ = [P_tok[x]]
                    if first:
                        w_ += cx.bank_free[ob] + cx.bank_free[db]
                        cx.bank_free[ob] = []
                        cx.bank_free[db] = []
                    pb = pbuf[x % 3]
                    E.op("tensor", lambda tt: tt.matmul(psum[:, ob, 0:nq], vtm[:, jb["kc"], jb["m"] * 128:(jb["m"] + 1) * 128],
                                                        pb[:, 0:nq], start=first, stop=lastj), w_, sig=False)
                    PV_tok[x] = E.op("tensor", lambda tt: tt.matmul(psum[:, db, 0:nq], ones[:], pb[:, 0:nq], start=first, stop=lastj),
                                     [ones_tok])
                    if lastj:
                        m, q0 = jb["m"], jb["q0"]
                        f1 = E.op("vector", lambda v: v.reciprocal(rden[:, 0:nq], psum[:, db, 0:nq]), [PV_tok[x], fin_last[0]])
                        f2 = E.op("vector", lambda v: v.tensor_tensor(t1[:, 0:nq], psum[:, ob, 0:nq], rden[:, 0:nq], ALU.mult), [f1, fin_last[0]])
                        f3 = E.op("gpsimd", lambda g: g.tensor_tensor(gst[:, m, q0:q0 + nq], t1[:, 0:nq], szT[:, m, q0:q0 + nq], ALU.mult),
                                  [f2, gst_free[0], proj_done[0]])
                        fin_last[0] = f3
                        cx.bank_free[ob] = [f2]
                        cx.bank_free[db] = [f1]
                emit_S(0)
                if nj > 1:
                    emit_S(1)
                for x in range(nj):
                    emit_elem(x)
                    emit_PV(x)
                    if x + 2 < nj:
                        emit_S(x + 2)
                gst_free[0] = E.dma("sync", gts_d.rearrange("p (a b) -> p a b", a=KC)[:, hg * 4:(hg + 1) * 4, :], gst, "gst",
                                    waits=[fin_last[0]])
            return [E.total("gst")]

        def pool_phase(i, r, hready):
            j = i // 2
            win = w_in_pool_d
            pm_free = [None]
            for g in range(4):
                pmw = [E.total("tensor"), pm_free[0]]
                if r == 0:
                    pm_tok = E.dma("gpsimd", pml, pml_d[g * 1280:(g + 1) * 1280, :].rearrange("(a p) n -> p a n", p=128), "ldc", waits=pmw)
                else:
                    pm_tok = E.dma("gpsimd", pmc, pmc_d[g * 256:(g + 1) * 256, :].rearrange("(a p) n -> p a n", p=128), "ldc", waits=pmw)
                for uu in range(NUG):
                    f0 = g * Dg + uu * UW
                    nfb = UW // 128
                    def lhsA(kc, t):
                        return A[:, kc, t * 128:(t + 1) * 128]
                    def ev_u(bi, tok):
                        return [copy_evac(utm[:, bi, :], psum[:, bi, 0:UW], tok)]
                    gemm_A(win, j * D, f0, UW, KC, lhsA, ev_u)
                    u_done = [E.total("scalar"), E.total("vector")]
                    for fb in range(nfb):
                        for tb in range(2):
                            bk = fb * 2 + tb
                            w_ = u_done + [pm_tok] + cx.bank_free[bk]
                            cx.bank_free[bk] = []
                            if r == 0:
                                kbs = _pml_kbs(tb)
                                for x, kb in enumerate(kbs):
                                    pt = E.op("tensor", lambda tt, x=x, kb=kb: tt.matmul(
                                        psum[:, bk, :], utm[:, kb, fb * 128:(fb + 1) * 128], pml[:, tb * 5 + x, :],
                                        start=(x == 0), stop=(x == len(kbs) - 1)), w_ if x == 0 else [], sig=(x == len(kbs) - 1))
                            else:
                                for sh in range(2):
                                    s = tb * 2 + sh
                                    for kb in range(2):
                                        pt = E.op("tensor", lambda tt, s=s, kb=kb, sh=sh: tt.matmul(
                                            psum[:, bk, sh * 256:(sh + 1) * 256], utm[:, 2 * s + kb, fb * 128:(fb + 1) * 128], pmc[:, kb, :],
                                            start=(kb == 0), stop=(kb == 1)), w_ if (sh == 0 and kb == 0) else [], sig=(sh == 1 and kb == 1))
                            cx.bank_free[bk] = [copy_evac(dT[:, uu * nfb + fb, tb * 512:(tb + 1) * 512], psum[:, bk, :], pt)]
                    def rhsA(kc, tb):
                        return A[:, kc, tb * 512:(tb + 1) * 512]
                    def ev_z(m, tb, bi, tok, uu=uu, nfb=nfb):
                        return [E.op("scalar", lambda s: s.activation(out=szg[:, uu * nfb + m, tb * 512:(tb + 1) * 512],
                                                                       in_=psum[:, bi, :], func=AF.Silu), [tok])]
                    gemm_B(win, j * D, D + f0, UW, KC, rhsA, ev_z)
                pm_free[0] = E.total("tensor")
                d_done = [E.total("scalar"), E.total("vector")]
                for uu in range(NUG):
                    nfb = UW // 128
                    def rhsD(kc, tb):
                        return dT[:, kc, tb * 512:(tb + 1) * 512]
                    def ev_y(m, tb, bi, tok, uu=uu, nfb=nfb):
                        fblk = g * NBG + uu * nfb + m
                        return [E.op("vector", lambda v: v.scalar_tensor_tensor(
                            gstp[:, m, tb * 512:(tb + 1) * 512], psum[:, bi, :], psfm[:, j, fblk:fblk + 1],
                            szg[:, uu * nfb + m, tb * 512:(tb + 1) * 512], ALU.mult, ALU.mult), [tok, gst_free[0]] + d_done)]
                    cx.bank_free[0] = cx.bank_free[0] + d_done
                    gemm_B(w_grp_d, (j * 4 + g) * Dg, uu * UW, UW, NBG, rhsD, ev_y)
                    nblk = UW // 128
                    b0 = g * NBG + uu * nfb
                    gst_free[0] = E.dma("sync", gts_d.rearrange("p (a b) -> p a b", a=KC)[:, b0:b0 + nblk, :], gstp[:, 0:nblk, :], "gst",
                                        waits=[E.total("vector")])
            return [E.total("gst")]

        def final_phase(r, ydst):
            xsrc = xs_d[r]
            fl = E.dma("sync", fgbc, fgbc_d, "ld0", waits=[E.total("vector"), E.total("gpsimd"), E.total("scalar")])
            xfree = [None, None]
            for t in range(NT):
                b = t % 2
                lt = E.dma("sync", xt[b], xsrc[t * 128:(t + 1) * 128, :], f"xt{b}",
                           waits=[xfree[b]] + ([E.total("xst0"), E.total("xst1"), E.total("gpsimd")] if t < 2 else []))
                a1 = E.op("scalar", lambda s: s.activation(out=junk, in_=xt[b], func=AF.Square, accum_out=ss[:, t:t + 1]), [lt])
                a2 = E.op("scalar", lambda s: s.activation(out=sd[:, t:t + 1], in_=ss[:, t:t + 1], func=AF.Sqrt,
                                                            bias=EPS_AP[0], scale=1.0 / D), [a1, eps_tok])
                v1 = E.op("vector", lambda v: v.reciprocal(rstd[:, t:t + 1], sd[:, t:t + 1]), [a2])
                v2 = E.op("vector", lambda v: v.scalar_tensor_tensor(xt[b], xt[b], rstd[:, t:t + 1], fgbc, ALU.mult, ALU.mult), [v1, fl])
                xfree[b] = E.dma("sync", ydst[t * 128:(t + 1) * 128, :], xt[b], f"fst{b}", waits=[v2])

        eps_tok = E.op("vector", lambda v: v.memset(EPS_AP[0], EPS), [])
        srcs = [xl_in, xc_in]
        for i in range(DEPTH):
            if i + 1 < DEPTH:
                ada_state["gen"] = ada_layer(i + 1)
            for r in range(2):
                xsrc = srcs[r] if i == 0 else xs_d[r]
                hready = norm_phase(i, r, xsrc)
                cx.bank_free[0] = cx.bank_free[0] + hready
                if DBG_STOP == 3:
                    return finish()
                if i % 2 == 0:
                    gready = attn_phase(i, r, hready)
                    w2d, r0 = w_out_attn_d, (i // 2) * D
                else:
                    gready = pool_phase(i, r, hready)
                    w2d, r0 = w_out_pool_d, (i // 2) * D
                if DBG_STOP == 4:
                    return finish()
                outproj_phase(i, r, w2d, r0, xsrc, gready + [E.total("tensor")])
                if DBG_STOP == 5:
                    return finish()
                if DBG_STOP == 6 and r == 1:
                    return finish()
                if DBG_STOP == 7 and r == 1 and i == 1:
                    return finish()
            ada_flush()
        for r in range(2):
            final_phase(r, [yl_d, yc_d][r])
        finish()

    EPS_AP = [sb("epsc", [128, 1], F32)[:]]
    TILE_LIST = {}
    cx0 = Cx(nc, sems, None, None)
    program(cx0)
    with nc.Block() as block:
        for eng in ENGS:
            def body(h, eng=eng):
                cx = Cx(nc, sems, eng, h)
                cx.tile_src = TILE_LIST
                program(cx)
            getattr(block, eng)(body)
    es.close()
    return nc


_NC_CACHE = {}


def _prep(inputs, D):
    f = lambda a: np.ascontiguousarray(np.asarray(a, dtype=np.float32))
    KC = D // 128
    x_prompt, x_sample = f(inputs["x_prompt"]), f(inputs["x_sample"])
    c, c_ctx = f(inputs["c"]), f(inputs["c_ctx"])
    cache_k, cache_v = f(inputs["cache_k"]), f(inputs["cache_v"])
    norm_g, b_ada = f(inputs["norm_g"]), f(inputs["b_ada"])
    m01, pml, pmc = _static_consts()
    fm = lambda v: np.ascontiguousarray(v.reshape(v.shape[0], -1, 128).transpose(2, 0, 1).reshape(128, -1))
    shared = {
        "ngfm": fm(norm_g),
        "fgbc": np.ascontiguousarray(np.broadcast_to(f(inputs["final_norm_g"])[None, :], (128, D))),
        "badafm": fm(b_ada),
        "badag": np.ascontiguousarray(np.broadcast_to(b_ada[:, 2 * D:].reshape(1, 4 * D), (2, 4 * D))),
        "psfm": fm(f(inputs["pool_scale"])),
        "rpbx": _rpb_expand(f(inputs["rpb"])).reshape(-1, NJ * 64),
        "identc": np.eye(128, dtype=np.float32),
        "onesc": np.ones((128, 128), np.float32),
        "pmlc": pml.reshape(-1, 512), "pmcc": pmc.reshape(-1, 256),
        "w_grp_pool": f(inputs["w_grp_pool"]).reshape(-1, D // 4),
    }
    BR = min(8, KC) * 128
    for n, cols in (("w_ada", 3 * D), ("w_in_attn", 4 * D), ("w_out_attn", D), ("w_in_pool", 2 * D), ("w_out_pool", D)):
        w = f(inputs[n]).reshape(-1, cols)
        for b in range(w.shape[0] // BR):
            shared[f"{n}_{b}"] = w[b * BR:(b + 1) * BR]
    maps = []
    for b in range(8):
        cond = np.stack([c[b], c_ctx], 0)
        m = dict(shared)
        m["xl"] = x_sample[b]
        m["xc"] = x_prompt[4 * b:4 * b + 4].reshape(T, D)
        m["condT"] = np.ascontiguousarray(cond.reshape(2, KC, 128).transpose(2, 1, 0).reshape(128, KC * 2))
        m["ck"] = cache_k[b].reshape(2 * PAST, D)
        m["cv"] = cache_v[b].reshape(2 * PAST, D)
        maps.append(m)
    return maps


def _run(inputs, D):
    if D not in _NC_CACHE:
        _NC_CACHE[D] = build(D)
    nc = _NC_CACHE[D]
    maps = _prep(inputs, D)
    res = run_bass_kernel_spmd(nc, maps, core_ids=list(range(8)))
    r = res.results
    y_sample = np.stack([r[b]["yl"] for b in range(8)], 0)
    y_prompt = np.concatenate([r[b]["yc"].reshape(4, 256, D) for b in range(8)], 0)
    H = D // 128
    nk = np.concatenate([r[b]["nk"].reshape(4, 2, 256, H, 128) for b in range(8)], 0)
    nv = np.concatenate([r[b]["nv"].reshape(4, 2, 256, H, 128) for b in range(8)], 0)
    return (y_prompt.astype(np.float32), y_sample.astype(np.float32), nk.astype(np.float32), nv.astype(np.float32))


def kernel(**inputs):
    return _run(inputs, 4096)
```

```python
import numpy as np
import concourse.bass as bass
import concourse.mybir as mybir
from concourse.bass_utils import run_bass_kernel_spmd

F32, BF16 = mybir.dt.float32, mybir.dt.bfloat16
AF = mybir.ActivationFunctionType
ALU = mybir.AluOpType

T = 1024
NT = 8
GRID_W, ROWS, WIN_H, WIN_W = 64, 16, 8, 16
PAST = 512
DEPTH = 4
EPS = 1e-6
NEG = -1e30
POOL_WINDOWS = (2, 4, 8, 16)
NS = 4
NJ = 22
ENGS = ("sync", "scalar", "vector", "gpsimd", "tensor")
DBG_STOP = 0


def _win_tiles():
    return [[0, 1, 2, 3, 4, 5], [2, 3, 4, 5, 6, 7]]


def _static_consts():
    r = np.arange(ROWS)
    rs = np.clip(r - WIN_H // 2, 0, ROWS - WIN_H)
    col = np.arange(GRID_W)
    cs = np.clip(col - WIN_W // 2, 0, GRID_W - WIN_W)
    col_in = (col[None, :] >= cs[:, None]) & (col[None, :] < cs[:, None] + WIN_W)
    m01 = np.zeros((12, 128, 512), np.float32)
    for qc in range(2):
        for ti, kc in enumerate(_win_tiles()[qc]):
            for krl in range(2):
                kr = 2 * kc + krl
                for qi in range(8):
                    qr = qc * 8 + qi
                    if rs[qr] <= kr < rs[qr] + WIN_H:
                        m01[qc * 6 + ti, krl * 64:(krl + 1) * 64, qi * 64:(qi + 1) * 64] = col_in.T
    def pm(w, L):
        t = np.arange(L)
        lo = np.clip(t - w // 2, 0, L)
        hi = np.clip(t + w // 2, 0, L)
        tp = np.arange(L)[:, None]
        m = ((tp >= lo[None, :]) & (tp < hi[None, :])).astype(np.float64) / (hi - lo)[None, :]
        m -= np.eye(L)
        return m.astype(np.float32)
    pml = np.zeros((4, 10, 128, 512), np.float32)
    pmc = np.zeros((4, 2, 128, 256), np.float32)
    for g, w in enumerate(POOL_WINDOWS):
        ml = pm(w, 1024)
        mc = pm(w, 256)
        for tb in range(2):
            for i, kb in enumerate(_pml_kbs(tb)):
                pml[g, tb * 5 + i] = ml[kb * 128:(kb + 1) * 128, tb * 512:(tb + 1) * 512]
        for kb in range(2):
            pmc[g, kb] = mc[kb * 128:(kb + 1) * 128, :]
    return m01, pml, pmc


def _pml_kbs(tb):
    return [0, 1, 2, 3, 4] if tb == 0 else [3, 4, 5, 6, 7]


def _rpb_expand(rpb):
    NL, H = rpb.shape[:2]
    krl = np.arange(2)[:, None, None, None]
    kcol = np.arange(64)[None, :, None, None]
    jj = np.arange(NJ)[None, None, :, None]
    qcol = np.arange(64)[None, None, None, :]
    drt = 10 + krl - jj
    cs = np.clip(qcol - WIN_W // 2, 0, GRID_W - WIN_W)
    colok = (kcol >= cs) & (kcol < cs + WIN_W)
    dri = np.clip(drt + WIN_H - 1, 0, 2 * WIN_H - 2)
    dci = np.clip(kcol - qcol, -(WIN_W - 1), WIN_W - 1) + WIN_W - 1
    shape = (2, 64, NJ, 64)
    dri, dci = np.broadcast_to(dri, shape), np.broadcast_to(dci, shape)
    g = rpb[:, :, dri, dci]
    outs = []
    for lo, hi in ((-(WIN_H // 2), WIN_H // 2 - 1), (-(WIN_H - 1), WIN_H - 1)):
        valid = np.broadcast_to((drt >= lo) & (drt <= hi) & colok, shape)
        outs.append(np.where(valid[None, None], g, np.float32(NEG)).astype(np.float32))
    out = np.stack(outs, 2)
    return np.ascontiguousarray(out.reshape(NL, H, 2, 128, NJ * 64))


def _win_variant(qc, kc):
    return 1 if (qc, kc) in ((0, 2), (0, 3), (1, 4), (1, 5)) else 0


class Cx:
    def __init__(self, nc, sems, cur, handle):
        self.nc, self.sems, self.cur, self.h = nc, sems, cur, handle
        self.cnt = {}
        self.waited = {}
        self.tile_idx = 0
        self.bank_free = [[] for _ in range(8)]
        self.slot_free = {}
        self.tile_src = {}

    def _wait(self, tok):
        if tok is None:
            return
        s, v = tok
        if self.waited.get(s, 0) >= v:
            return
        self.waited[s] = v
        self.h.wait_ge(self.sems[s], v)

    def op(self, eng, fn, waits=(), sig=True):
        tok = None
        if sig:
            self.cnt[eng] = self.cnt.get(eng, 0) + 1
            tok = (eng, self.cnt[eng])
        if eng == self.cur:
            for w in waits:
                self._wait(w)
            inst = fn(self.h)
            if sig:
                inst.then_inc(self.sems[eng], 1)
        return tok

    STORE_SEMS = ("ost", "gst", "xst0", "xst1", "fst0", "fst1", "gss")

    def dma(self, eng, out, in_, sem, waits=()):
        if eng == "sync" and sem not in self.STORE_SEMS:
            waits = list(waits) + [self.total(x) for x in self.STORE_SEMS]
        self.cnt[sem] = self.cnt.get(sem, 0) + 16
        tok = (sem, self.cnt[sem])
        if eng == self.cur:
            for w in waits:
                self._wait(w)
            self.h.dma_start(out=out, in_=in_).then_inc(self.sems[sem], 16)
        return tok

    def total(self, sem):
        c = self.cnt.get(sem, 0)
        return (sem, c) if c else None


SEM_NAMES = (list(ENGS) + [f"ring{i}" for i in range(NS)] +
             ["xt0", "xt1", "xst0", "xst1", "ld0", "ld1", "ldc", "cst", "bh0", "bh1", "gst", "ost", "fst0", "fst1", "misc", "ldo", "ldm", "ldi", "bhs", "gss"])


def build(D):
    KC = D // 128
    H = KC
    NU = D // 512
    KT = min(8, KC)
    Dg = D // 4
    NBG = Dg // 128
    UW = min(512, Dg)
    NUG = Dg // UW
    NHG = H // 4
    SCALE = 128.0 ** -0.5

    nc = bass.Bass("TRN2", target_bir_lowering=False)
    dt = lambda n, s, k="ExternalInput", d=F32: nc.dram_tensor(n, list(s), d, kind=k)
    xl_in = dt("xl", [T, D]).ap()
    xc_in = dt("xc", [T, D]).ap()
    condT_d = dt("condT", [128, KC * 2]).ap()
    ck_d = dt("ck", [2 * PAST, D]).ap()
    cv_d = dt("cv", [2 * PAST, D]).ap()
    ngfm_d = dt("ngfm", [128, 4 * KC]).ap()
    fgbc_d = dt("fgbc", [128, D]).ap()
    badafm_d = dt("badafm", [128, 4 * 3 * KC]).ap()
    badag_d = dt("badag", [2, 4 * D]).ap()
    psfm_d = dt("psfm", [128, 2 * KC]).ap()
    rpbx_d = dt("rpbx", [2 * H * 2 * 128, NJ * 64]).ap()
    ident_d = dt("identc", [128, 128]).ap()
    ones_d = dt("onesc", [128, 128]).ap()
    pml_d = dt("pmlc", [40 * 128, 512]).ap()
    pmc_d = dt("pmcc", [8 * 128, 256]).ap()
    BR = KT * 128
    blocks = lambda n, rows, cols: [dt(f"{n}_{b}", [BR, cols]).ap() for b in range(rows // BR)]
    w_ada_d = blocks("w_ada", 4 * D, 3 * D)
    w_in_attn_d = blocks("w_in_attn", 2 * D, 4 * D)
    w_out_attn_d = blocks("w_out_attn", 2 * D, D)
    w_in_pool_d = blocks("w_in_pool", 2 * D, 2 * D)
    w_grp_d = dt("w_grp_pool", [2 * 4 * Dg, Dg]).ap()
    w_out_pool_d = blocks("w_out_pool", 2 * D, D)
    yl_d = dt("yl", [T, D], "ExternalOutput").ap()
    yc_d = dt("yc", [T, D], "ExternalOutput").ap()
    nk_d = dt("nk", [4 * 2 * 256, D], "ExternalOutput").ap()
    nv_d = dt("nv", [4 * 2 * 256, D], "ExternalOutput").ap()
    xs_d = [dt("xls", [T, D], "Internal").ap(), dt("xcs", [T, D], "Internal").ap()]
    gts_d = dt("gts", [128, KC * T], "Internal", BF16).ap()
    gate_s = dt("gates", [4 * 2, D], "Internal").ap()

    ARENA = 96 * 1024
    from contextlib import ExitStack
    es = ExitStack()
    sb = lambda n, s, d: es.enter_context(nc.sbuf_tensor(n, list(s), d))
    A = sb("A", [128, KC, T], BF16)
    ring = sb("ring", [128, NS, KT, 512], BF16)
    arena = sb("arena", [128, ARENA // 2], BF16)
    ident = sb("ident", [128, 128], F32)
    ones = sb("ones", [128, 128], BF16)
    identb = sb("identb", [128, 128], BF16)
    condT = sb("condTs", [128, KC * 2], F32)
    scT = sb("scT", [128, KC, 2], BF16)
    shT = sb("shT", [128, 4, 2, KC], F32)
    g1sT = sb("g1sT", [128, 4, 2, KC], F32)
    ngfm = sb("ngfms", [128, 4, KC], F32)
    badafm = sb("badafms", [128, 4, 3 * KC], F32)
    bp1 = sb("bp1", [128, 4, KC], F32)
    psfm = sb("psfms", [128, 2, KC], F32)
    gsm = sb("gsm", [2, 512], F32)
    ss = sb("ss", [128, 8], F32)
    sd = sb("sd", [128, 8], F32)
    rstd = sb("rstd", [128, 8], F32)
    psum = es.enter_context(nc.psum_tensor("psum", [128, 8, 512], F32))
    sems = {n: es.enter_context(nc.semaphore(n)) for n in SEM_NAMES}

    def carve(off, shape, dtype):
        n = int(np.prod(shape))
        if dtype == BF16:
            v = arena[:, off // 2: off // 2 + n]
        else:
            v = arena[:, off // 2: off // 2 + 2 * n].bitcast(F32)
        if len(shape) == 1:
            return v
        names = " ".join(f"d{i}" for i in range(len(shape)))
        return v.rearrange(f"p ({names}) -> p {names}", **{f"d{i}": shape[i] for i in range(1, len(shape))})

    K = 1024
    xt = [carve(0, [D], F32), carve(16 * K, [D], F32)]
    xeb = [carve(0, [8, 512], F32), carve(16 * K, [8, 512], F32)]
    junk = carve(32 * K, [D], BF16)
    fgbc = carve(40 * K, [D], F32)
    gatebc = carve(32 * K, [D], F32)
    tmpe = [carve(48 * K, [512], F32), carve(50 * K, [512], F32)]
    gbias = carve(52 * K, [D], F32)
    qT = carve(0, [4, T], BF16)
    kT = carve(8 * K, [4, T + PAST], BF16)
    szT = carve(20 * K, [4, T], BF16)
    vtm = carve(28 * K, [12, 512], BF16)
    gst = carve(40 * K, [4, T], BF16)
    stmp = [carve(48 * K, [512], F32), carve(50 * K, [512], F32)]
    ebuf = [carve(52 * K, [512], BF16), carve(53 * K, [512], BF16)]
    pbuf = [carve(54 * K, [512], BF16), carve(55 * K, [512], BF16), carve(56 * K, [512], BF16)]
    rden = carve(57 * K, [512], F32)
    t1 = carve(59 * K, [512], F32)
    cst = carve(61 * K, [4, 512], F32)
    bhs = carve(69 * K, [2, NJ, 64], F32)
    bhb = [carve(69 * K + 2 * NJ * 256, [2, NJ * 64], BF16), carve(69 * K + 3 * NJ * 256, [2, NJ * 64], BF16)]
    ost = carve(61 * K, [8, 512], F32)
    kTf = carve(77 * K, [4, T], F32)
    utm = carve(0, [8, UW], BF16)
    dT = carve(8 * K, [NBG, T], BF16)
    szg = carve(24 * K, [NBG, T], BF16)
    gstp = carve(40 * K, [4, T], BF16)
    pml = carve(48 * K, [10, 512], BF16)
    pmc = carve(58 * K, [2, 256], BF16)
    gsb = carve(0, [D], F32)
    bg2 = carve(16 * K, [D], F32)

    def wview(w2d):
        return w2d.rearrange("(kc p) n -> p kc n", p=128)

    def program(cx):
        E = cx
        bank = lambda b: psum[:, b, :]

        def issue_tile(k):
            if k not in cx.tile_src:
                return
            src, nk, ncols = cx.tile_src[k]
            slot = k % NS
            E.dma("gpsimd", ring[:, slot, 0:nk, 0:ncols], src, f"ring{slot}",
                  waits=[cx.slot_free.get(k - NS)])

        def tile_tok(k):
            return (f"ring{k % NS}", 16 * (k // NS + 1))

        def gemm(w2d, r0, c0, ncols, nkc, naccs, mm_fn, evac_fn):
            blocked = isinstance(w2d, list)
            wv = None if blocked else wview(w2d)
            kc0 = r0 // 128
            ntile = (nkc + KT - 1) // KT
            acc_tok = [None] * naccs
            for kt in range(ntile):
                nk = min(KT, nkc - kt * KT)
                n = cx.tile_idx
                cx.tile_idx += 1
                if cx.cur is None:
                    if blocked:
                        src = wview(w2d[r0 // BR + kt])[:, 0:nk, c0:c0 + ncols]
                    else:
                        src = wv[:, kc0 + kt * KT: kc0 + kt * KT + nk, c0:c0 + ncols]
                    TILE_LIST[n] = (src, nk, ncols)
                if n == 0:
                    for k in range(NS):
                        issue_tile(k)
                slot = n % NS
                ttok = None
                for k in range(nk):
                    kc = kt * KT + k
                    for bi in range(naccs):
                        waits = []
                        if k == 0 and bi == 0:
                            waits.append(tile_tok(n))
                        if kc == 0:
                            waits += cx.bank_free[bi]
                            cx.bank_free[bi] = []
                        last_tile = (k == nk - 1 and bi == naccs - 1)
                        last_kc = (kc == nkc - 1)
                        tok = E.op("tensor",
                                   lambda t, bi=bi, kc=kc, k=k, slot=slot, last_kc=last_kc:
                                   mm_fn(t, bi, ring[:, slot, k, :], kc, kc == 0, last_kc),
                                   waits, sig=(last_tile or last_kc))
                        if last_kc:
                            acc_tok[bi] = tok
                        if last_tile:
                            ttok = tok
                cx.slot_free[n] = ttok
                issue_tile(n + NS)
            for bi in range(naccs):
                cx.bank_free[bi] = evac_fn(bi, acc_tok[bi])
            ada_tick()

        def gemm_B(w2d, r0, c0, ncols, nkc, rhs_fn, evac_fn, ntb=2, nfree=512):
            nm = ncols // 128
            def mm(t, bi, wt, kc, st, sp):
                m, tb = bi // ntb, bi % ntb
                return t.matmul(psum[:, bi, 0:nfree], wt[:, m * 128:(m + 1) * 128], rhs_fn(kc, tb), start=st, stop=sp)
            gemm(w2d, r0, c0, ncols, nkc, nm * ntb, mm, lambda bi, tok: evac_fn(bi // ntb, bi % ntb, bi, tok))

        def gemm_A(w2d, r0, c0, ncols, nkc, lhs_fn, evac_fn):
            def mm(t, bi, wt, kc, st, sp):
                return t.matmul(psum[:, bi, 0:ncols], lhs_fn(kc, bi), wt[:, 0:ncols], start=st, stop=sp)
            gemm(w2d, r0, c0, ncols, nkc, NT, mm, evac_fn)

        _alt = [0]
        def copy_evac(out, in_, tok, scale=None, extra=()):
            _alt[0] ^= 1
            if _alt[0]:
                if scale is None:
                    return E.op("vector", lambda v: v.tensor_copy(out, in_), [tok, *extra])
                return E.op("vector", lambda v: v.tensor_scalar(out, in_, scale, None, ALU.mult), [tok, *extra])
            return E.op("scalar", lambda s: s.activation(out=out, in_=in_, func=AF.Copy,
                                                          scale=(1.0 if scale is None else scale)), [tok, *extra])

        c_tok = []
        for o, i_ in ((ident[:], ident_d), (condT[:], condT_d), (ngfm[:].rearrange("p a b -> p (a b)"), ngfm_d),
                      (badafm[:].rearrange("p a b -> p (a b)"), badafm_d), (psfm[:].rearrange("p a b -> p (a b)"), psfm_d)):
            c_tok.append(E.dma("sync", o, i_, "misc"))
        c_all = E.total("misc")
        ones_tok = E.dma("gpsimd", ones[:], ones_d, "ldo")
        identb_tok = E.dma("gpsimd", identb[:], ident_d, "ldi")
        t_sc = E.op("scalar", lambda s: s.activation(out=scT[:].rearrange("p a b -> p (a b)"), in_=condT[:], func=AF.Silu), [c_all])
        t_bp1 = E.op("vector", lambda v: v.tensor_scalar(bp1[:], badafm[:, :, KC:2 * KC], 1.0, None, ALU.add), [c_all])

        def finish():
            if cx.cur == "sync":
                for s_ in Cx.STORE_SEMS:
                    cx._wait(E.total(s_))

        if DBG_STOP == 1:
            return finish()
        gsm_free = [None]
        cx.bank_free[0] = [t_sc]

        def ada_layer(i):
            wa = w_ada_d
            for u in range(2 * NU):
                def rhs_fn(kc, tb):
                    return scT[:, kc, :]
                def evac(m, tb, bi, tok, u=u, i=i):
                    blk = u * 4 + m
                    if blk < KC:
                        return [E.op("vector", lambda v: v.tensor_scalar(
                            shT[:, i, :, blk], psum[:, bi, 0:2], badafm[:, i, blk:blk + 1], None, ALU.add), [tok, c_all])]
                    c = blk - KC
                    return [E.op("vector", lambda v: v.tensor_scalar(
                        g1sT[:, i, :, c], psum[:, bi, 0:2], bp1[:, i, c:c + 1], ngfm[:, i, c:c + 1], ALU.add, ALU.mult),
                        [tok, t_bp1, c_all])]
                gemm_B(wa, i * D, u * 512, 512, KC, rhs_fn, evac, ntb=1, nfree=2)
                yield
            for u in range(NU):
                def mm(t, bi, wt, kc, st, sp):
                    return t.matmul(psum[0:2, bi, :], scT[:, kc, :], wt[:, 0:512], start=st, stop=sp)
                def evac(bi, tok, u=u):
                    tk = E.op("vector", lambda v: v.tensor_copy(gsm[0:2, :], psum[0:2, bi, :]), [tok, gsm_free[0]])
                    gsm_free[0] = E.dma("sync", gate_s[2 * i:2 * i + 2, u * 512:(u + 1) * 512], gsm[0:2, :], "gss", waits=[tk])
                    return [tk]
                gemm(wa, i * D, 2 * D + u * 512, 512, KC, 1, mm, evac)
                yield

        ada_state = {"gen": None, "n": 0, "busy": False}

        def ada_tick():
            if ada_state["busy"] or ada_state["gen"] is None:
                return
            ada_state["n"] += 1
            if ada_state["n"] % 2:
                return
            ada_state["busy"] = True
            if next(ada_state["gen"], "end") == "end":
                ada_state["gen"] = None
            ada_state["busy"] = False

        def ada_flush():
            ada_state["busy"] = True
            if ada_state["gen"] is not None:
                for _ in ada_state["gen"]:
                    pass
            ada_state["gen"] = None
            ada_state["busy"] = False

        ada_state["gen"] = ada_layer(0)
        ada_flush()
        mod_ready = [E.total("vector"), E.total("gst")]
        if DBG_STOP == 2:
            return finish()


        def norm_phase(i, r, xsrc):
            xfree = [None, None]
            v2s = {}

            def stage_a(t):
                b = t % 2
                lt = E.dma("sync", xt[b], xsrc[t * 128:(t + 1) * 128, :], f"xt{b}",
                           waits=[xfree[b]] + (([E.total("vector"), E.total("gst"), E.total("xst0"), E.total("xst1"), E.total("gpsimd")]) if t < 2 else []))
                a1 = E.op("scalar", lambda s: s.activation(out=junk, in_=xt[b], func=AF.Square,
                                                            accum_out=ss[:, t:t + 1]), [lt])
                a2 = E.op("scalar", lambda s: s.activation(out=sd[:, t:t + 1], in_=ss[:, t:t + 1], func=AF.Sqrt,
                                                            bias=EPS_AP[0], scale=1.0 / D), [a1, eps_tok])
                v1 = E.op("vector", lambda v: v.reciprocal(rstd[:, t:t + 1], sd[:, t:t + 1]), [a2])
                v2s[t] = E.op("vector", lambda v: v.tensor_scalar(xt[b], xt[b], rstd[:, t:t + 1], None, ALU.mult), [v1])

            def stage_b(t):
                b = t % 2
                v2 = v2s[t]
                for c4 in range(KC // 4):
                    bk = (t * (KC // 4) + c4) % 8
                    waits = [v2, c_all] + cx.bank_free[bk]
                    cx.bank_free[bk] = []
                    for q in range(4):
                        c = c4 * 4 + q
                        pt = E.op("tensor", lambda tt, c=c, q=q, bk=bk: tt.transpose(
                            psum[:, bk, q * 128:(q + 1) * 128], xt[b][:, c * 128:(c + 1) * 128], ident[:]),
                            waits if q == 0 else [], sig=(q == 3))
                    fr = []
                    for q in range(4):
                        c = c4 * 4 + q
                        o = A[:, c, t * 128:(t + 1) * 128]
                        i_ = psum[:, bk, q * 128:(q + 1) * 128]
                        if bk % 2 == 0:
                            fr.append(E.op("scalar", lambda s, o=o, i_=i_, c=c: s.activation(
                                out=o, in_=i_, func=AF.Identity, scale=g1sT[:, i, r, c:c + 1], bias=shT[:, i, r, c:c + 1]), [pt]))
                        else:
                            fr.append(E.op("vector", lambda v, o=o, i_=i_, c=c: v.tensor_scalar(
                                o, i_, g1sT[:, i, r, c:c + 1], shT[:, i, r, c:c + 1], ALU.mult, ALU.add), [pt]))
                    cx.bank_free[bk] = fr
                xfree[b] = pt

            stage_a(0)
            for t in range(NT):
                if t + 1 < NT:
                    stage_a(t + 1)
                stage_b(t)
            return [E.total("scalar"), E.total("vector")]

        def outproj_phase(i, r, w2d, r0, xsrc, gts_ready):
            xdst = xs_d[r]
            lt = E.dma("sync", A[:].rearrange("p a b -> p (a b)"), gts_d, "ld0", waits=gts_ready)
            gl0 = E.dma("sync", gatebc, bass.AP(gate_s.tensor, (2 * i + r) * D, [[0, 128], [1, D]]), "ld1",
                        waits=[E.total("vector"), E.total("gpsimd")])
            gb = E.dma("sync", gbias, bass.AP(badag_d.tensor, i * D, [[0, 128], [1, D]]), "ldm")
            gl = E.op("gpsimd", lambda g: g.tensor_tensor(gatebc, gatebc, gbias, ALU.add), [gl0, gb])
            cx.bank_free[0] = cx.bank_free[0] + [lt]
            xe_free = [None, None]
            xl_tok = {}
            def load_xe(cu):
                b = cu % 2
                xl_tok[cu] = E.dma("sync", xeb[b],
                                   xsrc.rearrange("(t p) n -> p t n", p=128)[:, :, cu * 512:(cu + 1) * 512],
                                   f"xt{b}", waits=[xe_free[b]])
            load_xe(0)
            st_toks = [None, None]
            fin = []
            for cu in range(NU):
                b = cu % 2
                if cu + 1 < NU:
                    load_xe(cu + 1)
                xe = xeb[b]
                def lhs_fn(kc, t):
                    return A[:, kc, t * 128:(t + 1) * 128]
                def evac(bi, tok, cu=cu, xe=xe):
                    tb_ = bi % 2
                    d1 = E.op("vector", lambda v: v.tensor_tensor(tmpe[tb_], psum[:, bi, :], gatebc[:, cu * 512:(cu + 1) * 512], ALU.mult),
                              [tok, gl, lt] + fin[-2:-1])
                    p1 = E.op("gpsimd", lambda g: g.tensor_tensor(xe[:, bi, :], tmpe[tb_], xe[:, bi, :], ALU.add), [d1, xl_tok[cu]])
                    fin.append(p1)
                    return [d1]
                gemm_A(w2d, r0, cu * 512, 512, KC, lhs_fn, evac)
                st = E.dma("sync", xdst.rearrange("(t p) n -> p t n", p=128)[:, :, cu * 512:(cu + 1) * 512], xe,
                           f"xst{b}", waits=[fin[-1]])
                xe_free[b] = st
            return [E.total("xst0"), E.total("xst1")]

        gst_free = [None]
        ost_free = [None]
        bh_free = [None, None]
        bhs_free = [None]

        def attn_phase(i, r, hready):
            j = i // 2
            win = w_in_attn_d
            cvt_prev = [None]
            bh_tok = {}
            for hg in range(NHG):
                if r == 0:
                    lk = E.dma("sync", cst, ck_d[j * PAST:(j + 1) * PAST, hg * 512:(hg + 1) * 512].rearrange("(c p) n -> p c n", p=128),
                               "cst", waits=[cvt_prev[0], E.total("tensor")] if hg == 0 else [cvt_prev[0]])
                def rhsA(kc, tb):
                    return A[:, kc, tb * 512:(tb + 1) * 512]
                def ev_q(m, tb, bi, tok):
                    return [copy_evac(qT[:, m, tb * 512:(tb + 1) * 512], psum[:, bi, :], tok, scale=SCALE)]
                def ev_k(m, tb, bi, tok):
                    if r == 1:
                        o16, o32, i_ = kT[:, m, tb * 512:(tb + 1) * 512], kTf[:, m, tb * 512:(tb + 1) * 512], psum[:, bi, :]
                        if bi % 2 == 0:
                            return [E.op("scalar", lambda s: s.activation(out=o16, in_=i_, func=AF.Copy), [tok]),
                                    E.op("scalar", lambda s: s.activation(out=o32, in_=i_, func=AF.Copy), [tok])]
                        return [E.op("vector", lambda v: v.tensor_copy(o16, i_), [tok]),
                                E.op("vector", lambda v: v.tensor_copy(o32, i_), [tok])]
                    return [copy_evac(kT[:, m, tb * 512:(tb + 1) * 512], psum[:, bi, :], tok)]
                def ev_z(m, tb, bi, tok):
                    return [E.op("scalar", lambda s: s.activation(out=szT[:, m, tb * 512:(tb + 1) * 512], in_=psum[:, bi, :], func=AF.Silu), [tok])]
                def lhsA(kc, t):
                    return A[:, kc, t * 128:(t + 1) * 128]
                def ev_v(bi, tok):
                    if r == 1:
                        if bi % 2 == 0:
                            return [E.op("scalar", lambda s: s.activation(out=vtm[:, bi, :], in_=psum[:, bi, :], func=AF.Copy), [tok]),
                                    E.op("scalar", lambda s: s.activation(out=ost[:, bi, :], in_=psum[:, bi, :], func=AF.Copy), [tok, ost_free[0]])]
                        return [E.op("vector", lambda v: v.tensor_copy(vtm[:, bi, :], psum[:, bi, :]), [tok]),
                                E.op("vector", lambda v: v.tensor_copy(ost[:, bi, :], psum[:, bi, :]), [tok, ost_free[0]])]
                    return [copy_evac(vtm[:, bi, :], psum[:, bi, :], tok)]
                def store_ost(dst):
                    toks = []
                    for s in range(4):
                        toks.append(E.dma("sync", dst[(s * 2 + j) * 256:(s * 2 + j + 1) * 256, hg * 512:(hg + 1) * 512]
                                          .rearrange("(t p) n -> p t n", p=128), ost[:, 2 * s:2 * s + 2, :], "ost",
                                          waits=[E.total("vector"), E.total("scalar")]))
                    ost_free[0] = toks[-1]
                    return toks
                gemm_B(win, j * D, hg * 512, 512, KC, rhsA, ev_q)
                gemm_B(win, j * D, D + hg * 512, 512, KC, rhsA, ev_k)
                k_done = [E.total("scalar"), E.total("vector")]
                gemm_B(win, j * D, 3 * D + hg * 512, 512, KC, rhsA, ev_z)
                gemm_A(win, j * D, 2 * D + hg * 512, 512, KC, lhsA, ev_v)
                if r == 1:
                    store_ost(nv_d)
                    def ev_kt(bi, tok):
                        if bi % 2 == 0:
                            return [E.op("scalar", lambda s: s.activation(out=ost[:, bi, :], in_=psum[:, bi, :], func=AF.Copy), [tok, ost_free[0]])]
                        return [E.op("vector", lambda v: v.tensor_copy(ost[:, bi, :], psum[:, bi, :]), [tok, ost_free[0]])]
                    for t in range(NT):
                        w_ = k_done + cx.bank_free[t]
                        cx.bank_free[t] = []
                        for m in range(4):
                            pt = E.op("tensor", lambda tt, m=m, t=t: tt.transpose(
                                psum[:, t, m * 128:(m + 1) * 128], kTf[:, m, t * 128:(t + 1) * 128], ident[:]),
                                w_ if m == 0 else [], sig=(m == 3))
                        cx.bank_free[t] = ev_kt(t, pt)
                    store_ost(nk_d)
                proj_done = [E.total("scalar"), E.total("vector")]
                jobs = []
                if r == 0:
                    for m in range(4):
                        bk = 7
                        w_ = [lk] + cx.bank_free[bk]
                        cx.bank_free[bk] = []
                        for c in range(4):
                            pt = E.op("tensor", lambda tt, c=c, m=m: tt.transpose(psum[:, 7, c * 128:(c + 1) * 128],
                                                                                  cst[:, c, m * 128:(m + 1) * 128], ident[:]),
                                      w_ if c == 0 else [], sig=(c == 3))
                        cx.bank_free[bk] = [copy_evac(kT[:, m, T:T + PAST], psum[:, 7, :], pt)]
                    kc_done = [E.total("scalar"), E.total("vector")]
                    lv = E.dma("sync", cst, cv_d[j * PAST:(j + 1) * PAST, hg * 512:(hg + 1) * 512].rearrange("(c p) n -> p c n", p=128),
                               "cst", waits=kc_done + [E.total("tensor")])
                    cvt = E.op("vector", lambda v: v.tensor_copy(vtm[:, 8:12, :], cst), [lv])
                    cvt_prev[0] = cvt
                    for m in range(4):
                        for qc in range(2):
                            for ti, kc in enumerate(_win_tiles()[qc]):
                                jobs.append(dict(m=m, g=(m, qc), kc=kc, q0=qc * 512, nq=512, win=qc * 6 + ti,
                                                 jj0=10 - 2 * kc + 8 * qc, ko=kc * 128, var=_win_variant(qc, kc)))
                            for c in range(4):
                                jobs.append(dict(m=m, g=(m, qc), kc=8 + c, q0=qc * 512, nq=512, win=None, ko=T + c * 128))
                else:
                    cvt = None
                    for m in range(4):
                        for s in range(4):
                            for c in range(2):
                                jobs.append(dict(m=m, g=(m, s), kc=2 * s + c, q0=s * 256, nq=256, win=None, ko=(2 * s + c) * 128))
                PV_tok, last_bh = {}, {}
                fin_last = [None]
                def load_bh(hh):
                    if hh in bh_tok or hh >= H:
                        return
                    r0_ = (j * H + hh) * 2 * 128
                    ld = E.dma("sync", bhs, rpbx_d[r0_:r0_ + 256, :].rearrange("(v p) (a b) -> p v a b", p=128, a=NJ), "bhs",
                               waits=[bhs_free[0]])
                    bh_tok[hh] = E.op("gpsimd", lambda g: g.tensor_copy(bhb[hh % 2], bhs.rearrange("p v a b -> p v (a b)")),
                                      [ld, bh_free[hh % 2]])
                    bhs_free[0] = bh_tok[hh]
                if r == 0:
                    load_bh(hg * 4)
                nj = len(jobs)
                S_tok = [None] * nj
                P_tok = [None] * nj
                grp_idx = {}
                for jb in jobs:
                    grp_idx.setdefault(jb["g"], len(grp_idx))
                def emit_S(x):
                    jb = jobs[x]
                    bk = x % 3
                    w_ = proj_done + cx.bank_free[bk] + ([cvt] if cvt else []) + ([E.total("scalar"), E.total("vector")] if x == 0 else [])
                    cx.bank_free[bk] = []
                    iswin = jb["win"] is not None
                    S_tok[x] = E.op("tensor", lambda tt: tt.matmul(psum[:, bk, 0:jb["nq"]], kT[:, jb["m"], jb["ko"]:jb["ko"] + 128],
                                                                   qT[:, jb["m"], jb["q0"]:jb["q0"] + jb["nq"]], start=True, stop=not iswin),
                                    w_, sig=not iswin)
                    if iswin:
                        m = jb["m"]
                        hh = hg * 4 + m
                        load_bh(hh)
                        load_bh(hh + 1)
                        c0_ = jb["jj0"] * 64
                        S_tok[x] = E.op("tensor", lambda tt: tt.matmul(psum[:, bk, :], identb[:], bhb[m % 2][:, jb["var"], c0_:c0_ + 512],
                                                                       start=False, stop=True), [bh_tok[hh], identb_tok])
                        bh_free[m % 2] = S_tok[x]
                def emit_elem(x):
                    jb = jobs[x]
                    bk = x % 3
                    nq = jb["nq"]
                    pb = pbuf[x % 3]
                    pfree = PV_tok.get(x - 3)
                    a1 = E.op("scalar", lambda s: s.activation(out=pb[:, 0:nq], in_=psum[:, bk, 0:nq], func=AF.Exp), [S_tok[x], pfree])
                    P_tok[x] = a1
                    cx.bank_free[bk] = [a1]
                def emit_PV(x):
                    jb = jobs[x]
                    gi = grp_idx[jb["g"]]
                    nq = jb["nq"]
                    ob, db = 3 + gi % 2, 5 + gi % 2
                    first = (x == 0 or jobs[x - 1]["g"] != jb["g"])
                    lastj = (x == nj - 1 or jobs[x + 1]["g"] != jb["g"])
                    w_ = [P_tok[x]]
                    if first:
                        w_ += cx.bank_free[ob] + cx.bank_free[db]
                        cx.bank_free[ob] = []
                        cx.bank_free[db] = []
                    pb = pbuf[x % 3]
                    E.op("tensor", lambda tt: tt.matmul(psum[:, ob, 0:nq], vtm[:, jb["kc"], jb["m"] * 128:(jb["m"] + 1) * 128],
                                                        pb[:, 0:nq], start=first, stop=lastj), w_, sig=False)
                    PV_tok[x] = E.op("tensor", lambda tt: tt.matmul(psum[:, db, 0:nq], ones[:], pb[:, 0:nq], start=first, stop=lastj),
                                     [ones_tok])
                    if lastj:
                        m, q0 = jb["m"], jb["q0"]
                        f1 = E.op("vector", lambda v: v.reciprocal(rden[:, 0:nq], psum[:, db, 0:nq]), [PV_tok[x], fin_last[0]])
                        f2 = E.op("vector", lambda v: v.tensor_tensor(t1[:, 0:nq], psum[:, ob, 0:nq], rden[:, 0:nq], ALU.mult), [f1, fin_last[0]])
                        f3 = E.op("gpsimd", lambda g: g.tensor_tensor(gst[:, m, q0:q0 + nq], t1[:, 0:nq], szT[:, m, q0:q0 + nq], ALU.mult),
                                  [f2, gst_free[0], proj_done[0]])
                        fin_last[0] = f3
                        cx.bank_free[ob] = [f2]
                        cx.bank_free[db] = [f1]
                emit_S(0)
                if nj > 1:
                    emit_S(1)
                for x in range(nj):
                    emit_elem(x)
                    emit_PV(x)
                    if x + 2 < nj:
                        emit_S(x + 2)
                gst_free[0] = E.dma("sync", gts_d.rearrange("p (a b) -> p a b", a=KC)[:, hg * 4:(hg + 1) * 4, :], gst, "gst",
                                    waits=[fin_last[0]])
            return [E.total("gst")]

        def pool_phase(i, r, hready):
            j = i // 2
            win = w_in_pool_d
            pm_free = [None]
            for g in range(4):
                pmw = [E.total("tensor"), pm_free[0]]
                if r == 0:
                    pm_tok = E.dma("gpsimd", pml, pml_d[g * 1280:(g + 1) * 1280, :].rearrange("(a p) n -> p a n", p=128), "ldc", waits=pmw)
                else:
                    pm_tok = E.dma("gpsimd", pmc, pmc_d[g * 256:(g + 1) * 256, :].rearrange("(a p) n -> p a n", p=128), "ldc", waits=pmw)
                for uu in range(NUG):
                    f0 = g * Dg + uu * UW
                    nfb = UW // 128
                    def lhsA(kc, t):
                        return A[:, kc, t * 128:(t + 1) * 128]
                    def ev_u(bi, tok):
                        return [copy_evac(utm[:, bi, :], psum[:, bi, 0:UW], tok)]
                    gemm_A(win, j * D, f0, UW, KC, lhsA, ev_u)
                    u_done = [E.total("scalar"), E.total("vector")]
                    for fb in range(nfb):
                        for tb in range(2):
                            bk = fb * 2 + tb
                            w_ = u_done + [pm_tok] + cx.bank_free[bk]
                            cx.bank_free[bk] = []
                            if r == 0:
                                kbs = _pml_kbs(tb)
                                for x, kb in enumerate(kbs):
                                    pt = E.op("tensor", lambda tt, x=x, kb=kb: tt.matmul(
                                        psum[:, bk, :], utm[:, kb, fb * 128:(fb + 1) * 128], pml[:, tb * 5 + x, :],
                                        start=(x == 0), stop=(x == len(kbs) - 1)), w_ if x == 0 else [], sig=(x == len(kbs) - 1))
                            else:
                                for sh in range(2):
                                    s = tb * 2 + sh
                                    for kb in range(2):
                                        pt = E.op("tensor", lambda tt, s=s, kb=kb, sh=sh: tt.matmul(
                                            psum[:, bk, sh * 256:(sh + 1) * 256], utm[:, 2 * s + kb, fb * 128:(fb + 1) * 128], pmc[:, kb, :],
                                            start=(kb == 0), stop=(kb == 1)), w_ if (sh == 0 and kb == 0) else [], sig=(sh == 1 and kb == 1))
                            cx.bank_free[bk] = [copy_evac(dT[:, uu * nfb + fb, tb * 512:(tb + 1) * 512], psum[:, bk, :], pt)]
                    def rhsA(kc, tb):
                        return A[:, kc, tb * 512:(tb + 1) * 512]
                    def ev_z(m, tb, bi, tok, uu=uu, nfb=nfb):
                        return [E.op("scalar", lambda s: s.activation(out=szg[:, uu * nfb + m, tb * 512:(tb + 1) * 512],
                                                                       in_=psum[:, bi, :], func=AF.Silu), [tok])]
                    gemm_B(win, j * D, D + f0, UW, KC, rhsA, ev_z)
                pm_free[0] = E.total("tensor")
                d_done = [E.total("scalar"), E.total("vector")]
                for uu in range(NUG):
                    nfb = UW // 128
                    def rhsD(kc, tb):
                        return dT[:, kc, tb * 512:(tb + 1) * 512]
                    def ev_y(m, tb, bi, tok, uu=uu, nfb=nfb):
                        fblk = g * NBG + uu * nfb + m
                        return [E.op("vector", lambda v: v.scalar_tensor_tensor(
                            gstp[:, m, tb * 512:(tb + 1) * 512], psum[:, bi, :], psfm[:, j, fblk:fblk + 1],
                            szg[:, uu * nfb + m, tb * 512:(tb + 1) * 512], ALU.mult, ALU.mult), [tok, gst_free[0]] + d_done)]
                    cx.bank_free[0] = cx.bank_free[0] + d_done
                    gemm_B(w_grp_d, (j * 4 + g) * Dg, uu * UW, UW, NBG, rhsD, ev_y)
                    nblk = UW // 128
                    b0 = g * NBG + uu * nfb
                    gst_free[0] = E.dma("sync", gts_d.rearrange("p (a b) -> p a b", a=KC)[:, b0:b0 + nblk, :], gstp[:, 0:nblk, :], "gst",
                                        waits=[E.total("vector")])
            return [E.total("gst")]

        def final_phase(r, ydst):
            xsrc = xs_d[r]
            fl = E.dma("sync", fgbc, fgbc_d, "ld0", waits=[E.total("vector"), E.total("gpsimd"), E.total("scalar")])
            xfree = [None, None]
            for t in range(NT):
                b = t % 2
                lt = E.dma("sync", xt[b], xsrc[t * 128:(t + 1) * 128, :], f"xt{b}",
                           waits=[xfree[b]] + ([E.total("xst0"), E.total("xst1"), E.total("gpsimd")] if t < 2 else []))
                a1 = E.op("scalar", lambda s: s.activation(out=junk, in_=xt[b], func=AF.Square, accum_out=ss[:, t:t + 1]), [lt])
                a2 = E.op("scalar", lambda s: s.activation(out=sd[:, t:t + 1], in_=ss[:, t:t + 1], func=AF.Sqrt,
                                                            bias=EPS_AP[0], scale=1.0 / D), [a1, eps_tok])
                v1 = E.op("vector", lambda v: v.reciprocal(rstd[:, t:t + 1], sd[:, t:t + 1]), [a2])
                v2 = E.op("vector", lambda v: v.scalar_tensor_tensor(xt[b], xt[b], rstd[:, t:t + 1], fgbc, ALU.mult, ALU.mult), [v1, fl])
                xfree[b] = E.dma("sync", ydst[t * 128:(t + 1) * 128, :], xt[b], f"fst{b}", waits=[v2])

        eps_tok = E.op("vector", lambda v: v.memset(EPS_AP[0], EPS), [])
        srcs = [xl_in, xc_in]
        for i in range(DEPTH):
            if i + 1 < DEPTH:
                ada_state["gen"] = ada_layer(i + 1)
            for r in range(2):
                xsrc = srcs[r] if i == 0 else xs_d[r]
                hready = norm_phase(i, r, xsrc)
                cx.bank_free[0] = cx.bank_free[0] + hready
                if DBG_STOP == 3:
                    return finish()
                if i % 2 == 0:
                    gready = attn_phase(i, r, hready)
                    w2d, r0 = w_out_attn_d, (i // 2) * D
                else:
                    gready = pool_phase(i, r, hready)
                    w2d, r0 = w_out_pool_d, (i // 2) * D
                if DBG_STOP == 4:
                    return finish()
                outproj_phase(i, r, w2d, r0, xsrc, gready + [E.total("tensor")])
                if DBG_STOP == 5:
                    return finish()
                if DBG_STOP == 6 and r == 1:
                    return finish()
                if DBG_STOP == 7 and r == 1 and i == 1:
                    return finish()
            ada_flush()
        for r in range(2):
            final_phase(r, [yl_d, yc_d][r])
        finish()

    EPS_AP = [sb("epsc", [128, 1], F32)[:]]
    TILE_LIST = {}
    cx0 = Cx(nc, sems, None, None)
    program(cx0)
    with nc.Block() as block:
        for eng in ENGS:
            def body(h, eng=eng):
                cx = Cx(nc, sems, eng, h)
                cx.tile_src = TILE_LIST
                program(cx)
            getattr(block, eng)(body)
    es.close()
    return nc


_NC_CACHE = {}


def _prep(inputs, D):
    f = lambda a: np.ascontiguousarray(np.asarray(a, dtype=np.float32))
    KC = D // 128
    x_prompt, x_sample = f(inputs["x_prompt"]), f(inputs["x_sample"])
    c, c_ctx = f(inputs["c"]), f(inputs["c_ctx"])
    cache_k, cache_v = f(inputs["cache_k"]), f(inputs["cache_v"])
    norm_g, b_ada = f(inputs["norm_g"]), f(inputs["b_ada"])
    m01, pml, pmc = _static_consts()
    fm = lambda v: np.ascontiguousarray(v.reshape(v.shape[0], -1, 128).transpose(2, 0, 1).reshape(128, -1))
    shared = {
        "ngfm": fm(norm_g),
        "fgbc": np.ascontiguousarray(np.broadcast_to(f(inputs["final_norm_g"])[None, :], (128, D))),
        "badafm": fm(b_ada),
        "badag": np.ascontiguousarray(np.broadcast_to(b_ada[:, 2 * D:].reshape(1, 4 * D), (2, 4 * D))),
        "psfm": fm(f(inputs["pool_scale"])),
        "rpbx": _rpb_expand(f(inputs["rpb"])).reshape(-1, NJ * 64),
        "identc": np.eye(128, dtype=np.float32),
        "onesc": np.ones((128, 128), np.float32),
        "pmlc": pml.reshape(-1, 512), "pmcc": pmc.reshape(-1, 256),
        "w_grp_pool": f(inputs["w_grp_pool"]).reshape(-1, D // 4),
    }
    BR = min(8, KC) * 128
    for n, cols in (("w_ada", 3 * D), ("w_in_attn", 4 * D), ("w_out_attn", D), ("w_in_pool", 2 * D), ("w_out_pool", D)):
        w = f(inputs[n]).reshape(-1, cols)
        for b in range(w.shape[0] // BR):
            shared[f"{n}_{b}"] = w[b * BR:(b + 1) * BR]
    maps = []
    for b in range(8):
        cond = np.stack([c[b], c_ctx], 0)
        m = dict(shared)
        m["xl"] = x_sample[b]
        m["xc"] = x_prompt[4 * b:4 * b + 4].reshape(T, D)
        m["condT"] = np.ascontiguousarray(cond.reshape(2, KC, 128).transpose(2, 1, 0).reshape(128, KC * 2))
        m["ck"] = cache_k[b].reshape(2 * PAST, D)
        m["cv"] = cache_v[b].reshape(2 * PAST, D)
        maps.append(m)
    return maps


def _run(inputs, D):
    if D not in _NC_CACHE:
        _NC_CACHE[D] = build(D)
    nc = _NC_CACHE[D]
    maps = _prep(inputs, D)
    res = run_bass_kernel_spmd(nc, maps, core_ids=list(range(8)))
    r = res.results
    y_sample = np.stack([r[b]["yl"] for b in range(8)], 0)
    y_prompt = np.concatenate([r[b]["yc"].reshape(4, 256, D) for b in range(8)], 0)
    H = D // 128
    nk = np.concatenate([r[b]["nk"].reshape(4, 2, 256, H, 128) for b in range(8)], 0)
    nv = np.concatenate([r[b]["nv"].reshape(4, 2, 256, H, 128) for b in range(8)], 0)
    return (y_prompt.astype(np.float32), y_sample.astype(np.float32), nk.astype(np.float32), nv.astype(np.float32))


def kernel(**inputs):
    return _run(inputs, 4096)
```
